# Optimizing a Trainium2 kernel written in Bass

```python
import math
import functools
import jax
import jax.numpy as jnp
from jax import lax
import numpy as np

D_MODEL = 1024
BATCH = 16
SEQ = 2048
DEPTH = 2

GRID_W = 64
CTX_LEN = 256
N_EVEN = (DEPTH + 1) // 2
N_ODD = DEPTH // 2
EPS = 1e-6

GLA_HEADS = 4
GLA_DK = D_MODEL // 16
GLA_DV = D_MODEL // 8
GLA_RANK = 16
GLA_TAU = 16.0
GLA_CHUNK = 64

ML_HEADS = 4
ML_D = D_MODEL // 8
ML_CONV = 3
ML_CHUNK = 64

D_INNER = 2 * D_MODEL
SSD_HEADDIM = 64
SSD_HEADS = D_INNER // SSD_HEADDIM
SSD_GROUPS = 4
SSD_HPG = SSD_HEADS // SSD_GROUPS
SSD_STATE = 128
SSD_CONV = 3
SSD_CHUNK = 128

D_FF = 2816
FFN_CONV = 3

GLA_QK = GLA_HEADS * GLA_DK
GLA_V = GLA_HEADS * GLA_DV
ML_W = ML_HEADS * ML_D
EVEN_SIZES = (GLA_QK, GLA_QK, GLA_V, GLA_V, ML_W, ML_W, ML_W, ML_W, 4 * ML_HEADS)
EVEN_IN = sum(EVEN_SIZES)
SSD_BC = SSD_GROUPS * SSD_STATE
SSD_CONV_CH = D_INNER + 2 * SSD_BC
ODD_SIZES = (D_INNER, SSD_CONV_CH, 2 * SSD_HEADS)
ODD_IN = sum(ODD_SIZES)

kernel_name = 'hybrid_gla_mlstm_ssd_prefix_dit'


def _split(a, sizes):
    return jnp.split(a, [int(s) for s in np.cumsum(sizes)[:-1]], axis=-1)


def rmsnorm(x, g):
    xf = x.astype(jnp.float32)
    y = xf * lax.rsqrt(jnp.mean(xf * xf, axis=-1, keepdims=True) + EPS)
    return (y * g.astype(jnp.float32)).astype(x.dtype)


def modulate(x, g, shift, scale):
    return rmsnorm(x, g) * (1 + scale) + shift


def dwconv1d(x, w, b):
    k_w = w.shape[0]
    t = x.shape[1]
    pad = k_w // 2
    xp = jnp.pad(x, ((0, 0), (pad, pad), (0, 0)))
    return sum(xp[:, j:j + t] * w[j] for j in range(k_w)) + b


def dwconv2d_grid(x, w, b, rows):
    bsz, t, ch = x.shape
    y = lax.conv_general_dilated(x.reshape(bsz, rows, GRID_W, ch), w[:, :, None, :], (1, 1), 'SAME',
                                 dimension_numbers=('NHWC', 'HWIO', 'NHWC'), feature_group_count=ch)
    return y.reshape(bsz, t, ch) + b


def to_chunks(a, size):
    bsz, t = a.shape[:2]
    return jnp.moveaxis(a.reshape(bsz, t // size, size, *a.shape[2:]), 1, 0)


def from_chunks(a):
    a = jnp.moveaxis(a, 0, 1)
    return a.reshape(a.shape[0], a.shape[1] * a.shape[2], *a.shape[3:])


def gla_scan(q, k, v, log_a, s0, with_out):
    size = GLA_CHUNK
    tri = jnp.tril(jnp.ones((size, size), dtype=bool))[None, :, :, None, None]

    def step(s, inp):
        qc, kc, vc, lac = inp
        cum = jnp.cumsum(lac.astype(jnp.float32), axis=1)
        c_end = cum[:, -1]
        k_end = kc * jnp.exp(c_end[:, None] - cum)
        s_new = jnp.exp(c_end)[..., None] * s + jnp.einsum('bshk,bshv->bhkv', k_end, vc)
        if not with_out:
            return s_new, None
        rel = jnp.where(tri, cum[:, :, None] - cum[:, None, :], -jnp.inf)
        att = jnp.einsum('bthk,bshk,btshk->bhts', qc, kc, jnp.exp(rel))
        o = jnp.einsum('bhts,bshv->bthv', att, vc) + jnp.einsum('bthk,bhkv->bthv', qc * jnp.exp(cum), s)
        return s_new, o

    s, o = lax.scan(step, s0, tuple(to_chunks(a, size) for a in (q, k, v, log_a)))
    return (from_chunks(o) if with_out else None), s


def mlstm_scan(q, k, v, li, lf, state0, with_out):
    size = ML_CHUNK
    tri = jnp.tril(jnp.ones((size, size), dtype=bool))[None, :, :, None]

    def step(carry, inp):
        cmat, nvec, m = carry
        qc, kc, vc, lic, lfc = inp
        b = jnp.cumsum(lfc, axis=1)
        b_end = b[:, -1]
        w_end = b_end[:, None] - b + lic
        m_new = jnp.maximum(b_end + m, jnp.max(w_end, axis=1))
        decay = jnp.exp(b_end + m - m_new)
        ws = jnp.exp(w_end - m_new[:, None])
        c_new = decay[..., None, None] * cmat + jnp.einsum('bsh,bshk,bshv->bhkv', ws, kc, vc)
        n_new = decay[..., None] * nvec + jnp.einsum('bsh,bshk->bhk', ws, kc)
        if not with_out:
            return (c_new, n_new, m_new), None
        a_in = b + m[:, None]
        d_log = jnp.where(tri, b[:, :, None] - b[:, None, :] + lic[:, None, :], -jnp.inf)
        m_t = jnp.maximum(a_in, jnp.max(d_log, axis=2))
        dw = jnp.exp(d_log - m_t[:, :, None])
        aw = jnp.exp(a_in - m_t)
        sc = jnp.einsum('bthk,bshk->btsh', qc, kc) * dw
        num = aw[..., None] * jnp.einsum('bthk,bhkv->bthv', qc, cmat) + jnp.einsum('btsh,bshv->bthv', sc, vc)
        den = aw * jnp.einsum('bthk,bhk->bth', qc, nvec) + jnp.sum(sc, axis=2)
        h = num / jnp.maximum(jnp.abs(den), jnp.exp(-m_t))[..., None]
        return (c_new, n_new, m_new), h

    state, h = lax.scan(step, state0, tuple(to_chunks(a, size) for a in (q, k, v, li, lf)))
    return (from_chunks(h) if with_out else None), state


def ssd_scan(x, dt, bm, cm, s0, with_out, A):
    size = SSD_CHUNK
    tri = jnp.tril(jnp.ones((size, size), dtype=bool))[None, :, :, None, None]

    def step(s, inp):
        xc, dtc, bc, cc = inp
        cum = jnp.cumsum(dtc * A, axis=1)
        c_end = cum[:, -1]
        w_end = jnp.exp(c_end[:, None] - cum) * dtc
        s_new = jnp.exp(c_end)[..., None, None] * s + jnp.einsum('bsgh,bsgn,bsghp->bghpn', w_end, bc, xc)
        if not with_out:
            return s_new, None
        seg = jnp.where(tri, cum[:, :, None] - cum[:, None, :], -jnp.inf)
        cb = jnp.einsum('btgn,bsgn->btsg', cc, bc)
        w = jnp.exp(seg) * cb[..., None] * dtc[:, None]
        y = (jnp.einsum('btsgh,bsghp->btghp', w, xc)
             + jnp.exp(cum)[..., None] * jnp.einsum('btgn,bghpn->btghp', cc, s))
        return s_new, y

    s, y = lax.scan(step, s0, tuple(to_chunks(a, size) for a in (x, dt, bm, cm)))
    return (from_chunks(y) if with_out else None), s


def bidir_scan(scan_f, scan_b, ctx_f, lat_f, ctx_b, lat_b, init, ctx_out):
    def one_dir(scan_fn, ctx_args, lat_args):
        yc, s_ctx = scan_fn(*ctx_args, init, ctx_out)
        yl, _ = scan_fn(*lat_args, s_ctx, True)
        return yc, yl

    def rev(a):
        return jnp.flip(a, axis=1)

    yc_f, yl_f = one_dir(scan_f, ctx_f, lat_f)
    yc_b, yl_b = one_dir(scan_b, [rev(a) for a in ctx_b], [rev(a) for a in lat_b])
    yl = yl_f + rev(yl_b)
    yc = (yc_f + rev(yc_b)) if ctx_out else None
    return yc, yl


def even_features(h, w_in, a1, a2, ab, conv_w, conv_b, gate_b):
    bsz, t, _ = h.shape
    gq, gk, gv, gg, mq, mk, mv, mo, mg = _split(h @ w_in, EVEN_SIZES)
    log_a = [(jax.nn.log_sigmoid(((h @ a1[d]) @ a2[d] + ab[d]).astype(jnp.float32)) / GLA_TAU)
             .reshape(bsz, t, GLA_HEADS, GLA_DK) for d in range(2)]
    qk = jax.nn.silu(dwconv1d(jnp.concatenate([mq, mk], axis=-1), conv_w, conv_b))
    mq, mk = jnp.split(qk, 2, axis=-1)
    gates = (mg + gate_b).astype(jnp.float32).reshape(bsz, t, 4, ML_HEADS)
    return {
        'gla_q': gq.reshape(bsz, t, GLA_HEADS, GLA_DK) * GLA_DK ** -0.5,
        'gla_k': gk.reshape(bsz, t, GLA_HEADS, GLA_DK),
        'gla_v': gv.reshape(bsz, t, GLA_HEADS, GLA_DV),
        'gla_g': gg,
        'la_f': log_a[0], 'la_b': log_a[1],
        'ml_q': mq.reshape(bsz, t, ML_HEADS, ML_D),
        'ml_k': mk.reshape(bsz, t, ML_HEADS, ML_D) * ML_D ** -0.5,
        'ml_v': mv.reshape(bsz, t, ML_HEADS, ML_D),
        'ml_o': mo,
        'li_f': gates[:, :, 0], 'lf_f': jax.nn.log_sigmoid(gates[:, :, 1]),
        'li_b': gates[:, :, 2], 'lf_b': jax.nn.log_sigmoid(gates[:, :, 3]),
    }


def even_combine(f, o_gla, h_ml, gla_norm_g, ml_norm_g, w_out):
    bsz, t = o_gla.shape[:2]
    gla = rmsnorm(o_gla, gla_norm_g.reshape(GLA_HEADS, GLA_DV)).reshape(bsz, t, GLA_V) * jax.nn.silu(f['gla_g'])
    ml = jax.nn.sigmoid(f['ml_o']).reshape(bsz, t, ML_HEADS, ML_D) * h_ml
    ml = rmsnorm(ml, ml_norm_g.reshape(ML_HEADS, ML_D)).reshape(bsz, t, ML_W)
    return jnp.concatenate([gla, ml], axis=-1).astype(w_out.dtype) @ w_out


def even_mixer(hc, hl, w_in, a1, a2, ab, conv_w, conv_b, gate_b, gla_norm_g, ml_norm_g, w_out, ctx_out):
    fc = even_features(hc, w_in, a1, a2, ab, conv_w, conv_b, gate_b)
    fl = even_features(hl, w_in, a1, a2, ab, conv_w, conv_b, gate_b)
    bsz = hl.shape[0]

    def g_args(f, d):
        return (f['gla_q'], f['gla_k'], f['gla_v'], f['la_' + d])

    gla_init = jnp.zeros((bsz, GLA_HEADS, GLA_DK, GLA_DV), jnp.float32)
    oc_gla, ol_gla = bidir_scan(gla_scan, gla_scan, g_args(fc, 'f'), g_args(fl, 'f'),
                                g_args(fc, 'b'), g_args(fl, 'b'), gla_init, ctx_out)

    def m_args(f, d):
        return (f['ml_q'], f['ml_k'], f['ml_v'], f['li_' + d], f['lf_' + d])

    ml_init = (jnp.zeros((bsz, ML_HEADS, ML_D, ML_D), jnp.float32),
               jnp.zeros((bsz, ML_HEADS, ML_D), jnp.float32),
               jnp.zeros((bsz, ML_HEADS), jnp.float32))
    hc_ml, hl_ml = bidir_scan(mlstm_scan, mlstm_scan, m_args(fc, 'f'), m_args(fl, 'f'),
                              m_args(fc, 'b'), m_args(fl, 'b'), ml_init, ctx_out)
    yl = even_combine(fl, ol_gla, hl_ml, gla_norm_g, ml_norm_g, w_out)
    yc = even_combine(fc, oc_gla, hc_ml, gla_norm_g, ml_norm_g, w_out) if ctx_out else None
    return yc, yl


def odd_features(h, w_in, conv_w, conv_b, dt_bias):
    bsz, t, _ = h.shape
    z, xbc, dt = _split(h @ w_in, ODD_SIZES)
    xs, bm, cm = _split(jax.nn.silu(dwconv1d(xbc, conv_w, conv_b)), (D_INNER, SSD_BC, SSD_BC))
    dt = jax.nn.softplus(dt.astype(jnp.float32).reshape(bsz, t, 2, SSD_HEADS) + dt_bias.astype(jnp.float32))
    dt = dt.reshape(bsz, t, 2, SSD_GROUPS, SSD_HPG)
    return {
        'z': z,
        'x': xs.reshape(bsz, t, SSD_GROUPS, SSD_HPG, SSD_HEADDIM),
        'b': bm.reshape(bsz, t, SSD_GROUPS, SSD_STATE),
        'c': cm.reshape(bsz, t, SSD_GROUPS, SSD_STATE),
        'dt_f': dt[:, :, 0], 'dt_b': dt[:, :, 1],
    }


def odd_mixer(hc, hl, w_in, conv_w, conv_b, dt_bias, a_log, d_skip, norm_g, w_out, ctx_out):
    fc = odd_features(hc, w_in, conv_w, conv_b, dt_bias)
    fl = odd_features(hl, w_in, conv_w, conv_b, dt_bias)
    bsz = hl.shape[0]
    A = -jnp.exp(a_log.astype(jnp.float32)).reshape(2, SSD_GROUPS, SSD_HPG)
    scan_f = functools.partial(ssd_scan, A=A[0])
    scan_b = functools.partial(ssd_scan, A=A[1])

    def s_args(f, d):
        return (f['x'], f['dt_' + d], f['b'], f['c'])

    init = jnp.zeros((bsz, SSD_GROUPS, SSD_HPG, SSD_HEADDIM, SSD_STATE), jnp.float32)
    yc, yl = bidir_scan(scan_f, scan_b, s_args(fc, 'f'), s_args(fl, 'f'),
                        s_args(fc, 'b'), s_args(fl, 'b'), init, ctx_out)

    def combine(f, y):
        b_, t_ = y.shape[:2]
        y = (y + d_skip.reshape(SSD_GROUPS, SSD_HPG)[..., None] * f['x']).reshape(b_, t_, D_INNER)
        y = (y * jax.nn.silu(f['z'])).reshape(b_, t_, SSD_GROUPS, D_INNER // SSD_GROUPS)
        y = rmsnorm(y, norm_g.reshape(SSD_GROUPS, D_INNER // SSD_GROUPS)).reshape(b_, t_, D_INNER)
        return y.astype(w_out.dtype) @ w_out

    return (combine(fc, yc) if ctx_out else None), combine(fl, yl)


def conv_ffn(h, w_up, conv_w, conv_b, w_down, rows):
    u = h @ w_up
    if rows is None:
        u = dwconv1d(u, conv_w[1], conv_b)
    else:
        u = dwconv2d_grid(u, conv_w, conv_b, rows)
    a, g = jnp.split(u, 2, axis=-1)
    return (jax.nn.silu(g) * a) @ w_down


def setup_inputs(seed: int = 0) -> dict:
    key = jax.random.key(seed)
    ks = list(jax.random.split(key, 40))
    f32 = jnp.float32

    def nrm(shape, scale):
        return scale * jax.random.normal(ks.pop(), shape, f32)

    def gain(shape):
        return 1.0 + 0.05 * jax.random.normal(ks.pop(), shape, f32)

    gate_base = jnp.concatenate([jnp.zeros((ML_HEADS,), f32), jnp.linspace(3.0, 6.0, ML_HEADS, dtype=f32)] * 2)
    dt0 = jnp.exp(jax.random.uniform(ks.pop(), (N_ODD, 2, SSD_HEADS), f32, math.log(1e-3), math.log(1e-1)))
    return {
        'x': nrm((BATCH, SEQ, D_MODEL), 1.0),
        'c': nrm((BATCH, D_MODEL), 1.0),
        'ctx': nrm((BATCH, CTX_LEN, D_MODEL), 1.0),
        'c_ctx': nrm((D_MODEL,), 1.0),
        'mod_w': nrm((DEPTH, D_MODEL, 6 * D_MODEL), 0.5 * D_MODEL ** -0.5),
        'mod_b': nrm((DEPTH, 6 * D_MODEL), 0.02),
        'norm_mix_g': gain((DEPTH, D_MODEL)),
        'norm_ffn_g': gain((DEPTH, D_MODEL)),
        'final_norm_g': gain((D_MODEL,)),
        'ffn_w_up': nrm((DEPTH, D_MODEL, 2 * D_FF), D_MODEL ** -0.5),
        'ffn_conv_w': nrm((DEPTH, FFN_CONV, FFN_CONV, 2 * D_FF), 1.0 / FFN_CONV),
        'ffn_conv_b': nrm((DEPTH, 2 * D_FF), 0.02),
        'ffn_w_down': nrm((DEPTH, D_FF, D_MODEL), D_FF ** -0.5),
        'even_w_in': nrm((N_EVEN, D_MODEL, EVEN_IN), D_MODEL ** -0.5),
        'gla_a1': nrm((N_EVEN, 2, D_MODEL, GLA_RANK), D_MODEL ** -0.5),
        'gla_a2': nrm((N_EVEN, 2, GLA_RANK, GLA_QK), GLA_RANK ** -0.5),
        'gla_ab': nrm((N_EVEN, 2, GLA_QK), 0.1),
        'ml_conv_w': nrm((N_EVEN, ML_CONV, 2 * ML_W), ML_CONV ** -0.5),
        'ml_conv_b': nrm((N_EVEN, 2 * ML_W), 0.02),
        'ml_gate_b': gate_base[None] + nrm((N_EVEN, 4 * ML_HEADS), 0.1),
        'gla_norm_g': gain((N_EVEN, GLA_V)),
        'ml_norm_g': gain((N_EVEN, ML_W)),
        'even_w_out': nrm((N_EVEN, GLA_V + ML_W, D_MODEL), (GLA_V + ML_W) ** -0.5),
        'ssd_w_in': nrm((N_ODD, D_MODEL, ODD_IN), D_MODEL ** -0.5),
        'ssd_conv_w': nrm((N_ODD, SSD_CONV, SSD_CONV_CH), SSD_CONV ** -0.5),
        'ssd_conv_b': nrm((N_ODD, SSD_CONV_CH), 0.02),
        'ssd_dt_bias': dt0 + jnp.log(-jnp.expm1(-dt0)),
        'ssd_a_log': jnp.log(jax.random.uniform(ks.pop(), (N_ODD, 2, SSD_HEADS), f32, 1.0, 16.0)),
        'ssd_d': gain((N_ODD, SSD_HEADS)),
        'ssd_norm_g': gain((N_ODD, D_INNER)),
        'ssd_w_out': nrm((N_ODD, D_INNER, D_MODEL), D_INNER ** -0.5),
    }


def reference(x, c, ctx, c_ctx, mod_w, mod_b, norm_mix_g, norm_ffn_g, final_norm_g,
              ffn_w_up, ffn_conv_w, ffn_conv_b, ffn_w_down,
              even_w_in, gla_a1, gla_a2, gla_ab, ml_conv_w, ml_conv_b, ml_gate_b,
              gla_norm_g, ml_norm_g, even_w_out,
              ssd_w_in, ssd_conv_w, ssd_conv_b, ssd_dt_bias, ssd_a_log, ssd_d, ssd_norm_g, ssd_w_out):
    rows = x.shape[1] // GRID_W
    xl, xc = x, ctx
    for layer in range(DEPTH):
        last = layer == DEPTH - 1
        mod_l = (jax.nn.silu(c) @ mod_w[layer] + mod_b[layer])[:, None, :]
        mod_c = (jax.nn.silu(c_ctx) @ mod_w[layer] + mod_b[layer])[None, None, :]
        sh1l, sc1l, g1l, sh2l, sc2l, g2l = jnp.split(mod_l, 6, axis=-1)
        sh1c, sc1c, g1c, sh2c, sc2c, g2c = jnp.split(mod_c, 6, axis=-1)
        hl = modulate(xl, norm_mix_g[layer], sh1l, sc1l)
        hc = modulate(xc, norm_mix_g[layer], sh1c, sc1c)
        if layer % 2 == 0:
            e = layer // 2
            yc, yl = even_mixer(hc, hl, even_w_in[e], gla_a1[e], gla_a2[e], gla_ab[e], ml_conv_w[e], ml_conv_b[e],
                                ml_gate_b[e], gla_norm_g[e], ml_norm_g[e], even_w_out[e], not last)
        else:
            o = layer // 2
            yc, yl = odd_mixer(hc, hl, ssd_w_in[o], ssd_conv_w[o], ssd_conv_b[o], ssd_dt_bias[o], ssd_a_log[o],
                               ssd_d[o], ssd_norm_g[o], ssd_w_out[o], not last)
        xl = xl + g1l * yl
        hl = modulate(xl, norm_ffn_g[layer], sh2l, sc2l)
        xl = xl + g2l * conv_ffn(hl, ffn_w_up[layer], ffn_conv_w[layer], ffn_conv_b[layer], ffn_w_down[layer], rows)
        if not last:
            xc = xc + g1c * yc
            hc = modulate(xc, norm_ffn_g[layer], sh2c, sc2c)
            xc = xc + g2c * conv_ffn(hc, ffn_w_up[layer], ffn_conv_w[layer], ffn_conv_b[layer], ffn_w_down[layer], None)
    return rmsnorm(xl, final_norm_g)
```

```python
import contextlib
import math
import os
import numpy as np
KDBG = set(os.environ.get("KDBG", "").split(","))
KLVL = int(os.environ.get("KLVL", "99"))
import concourse.bass as bass
import concourse.mybir as mybir
from concourse.bass_utils import run_bass_kernel_spmd

F32 = mybir.dt.float32
BF16 = mybir.dt.bfloat16
AF = mybir.ActivationFunctionType
ALU = mybir.AluOpType
AX = mybir.AxisListType

ENGS = ("pe", "act", "dve", "pool", "sp")
EPOCH = 30000
NDSEM = 24


def _prod(xs):
    r = 1
    for x in xs:
        r *= int(x)
    return r


def region(ap):
    t = ap.tensor
    cls = type(t).__name__
    if cls.startswith("DRam"):
        if not t.name.startswith("scr_"):
            return None
        ext = 1
        for st_, c_ in ap.ap:
            ext += (int(c_) - 1) * abs(int(st_))
        return (t.name, 0, 1, int(ap.offset), int(ap.offset) + ext)
    F = _prod(list(t.shape)[1:])
    off = int(ap.offset)
    p0 = off // F
    f0 = off % F
    dims = [(int(s), int(c)) for s, c in ap.ap]
    pn = 1
    if dims and dims[0][0] > 0 and dims[0][0] % F == 0:
        pn = (dims[0][1] - 1) * (dims[0][0] // F) + 1
        dims = dims[1:]
    elif dims and dims[0][1] == 1:
        dims = dims[1:]
    ext = 1
    for s, c in dims:
        ext += (c - 1) * abs(s)
    if cls.startswith("PSum"):
        return (t.name, p0, p0 + pn, 0, F)
    return (t.name, p0, p0 + pn, f0, f0 + ext)


class Op:
    __slots__ = ("eng", "fn", "deps", "dma", "marked", "sem", "val", "idx")


class Sched:
    def __init__(self, nc, same_engine_sync=True):
        self.nc = nc
        self.ops = []
        self.recs = {}
        self.same = same_engine_sync
        self.ndma = 0
        self.dma_ops = []

    def add(self, eng, fn, outs, ins, dma=False):
        op = Op()
        op.eng = eng
        op.fn = fn
        op.dma = dma
        op.marked = False
        op.idx = len(self.ops)
        deps = set()
        rr = [region(a) for a in ins]
        ww = [region(a) for a in outs]
        for r in rr:
            if r is None:
                continue
            psum = r[0].startswith("ps")
            for rec in self.recs.get(r[0], ()):
                if (rec[6] or (psum and rec[7] != eng)) and rec[1] < r[2] and r[1] < rec[2] and rec[3] < r[4] and r[3] < rec[4]:
                    deps.add(rec[5])
        for w in ww:
            if w is None:
                continue
            for rec in self.recs.get(w[0], ()):
                if rec[1] < w[2] and w[1] < rec[2] and rec[3] < w[4] and w[3] < rec[4]:
                    deps.add(rec[5])
        if dma:
            if self.ndma >= NDSEM:
                deps.add(self.dma_ops[self.ndma - NDSEM])
            self.dma_ops.append(op.idx)
            self.ndma += 1
        fd = set()
        for d in deps:
            o = self.ops[d]
            if o.eng == eng and not o.dma:
                if eng == "pe" or eng == "sp" or not self.same:
                    continue
            fd.add(d)
        op.deps = fd
        for d in fd:
            self.ops[d].marked = True
        self.ops.append(op)
        for w in ww:
            if w is None:
                continue
            lst = self.recs.setdefault(w[0], [])
            lst[:] = [rec for rec in lst if not (w[1] <= rec[1] and rec[2] <= w[2] and w[3] <= rec[3] and rec[4] <= w[4])]
            lst.append((w[0], w[1], w[2], w[3], w[4], op.idx, True, eng))
        for r in rr:
            if r is None:
                continue
            lst = self.recs.setdefault(r[0], [])
            lst[:] = [rec for rec in lst if not ((not rec[6]) and rec[7] == eng and not dma and not self.ops[rec[5]].dma
                                                 and r[1] <= rec[1] and rec[2] <= r[2] and r[3] <= rec[3] and rec[4] <= r[4])]
            lst.append((r[0], r[1], r[2], r[3], r[4], op.idx, False, eng))
        return op

    def call(self, eng, meth, **kw):
        outs, ins = [], []
        for k, v in kw.items():
            if isinstance(v, bass.AP):
                (outs if k in ("out", "accum_out", "ap") else ins).append(v)
        return self.add(eng, lambda e, m=meth, k=kw: getattr(e, m)(**k), outs, ins)

    def dma(self, eng, out, in_, **kw):
        return self.add(eng, lambda e, o=out, i=in_, k=kw: e.dma_start(out=o, in_=i, **k), [out], [in_], dma=True)

    def mm(self, out, lhsT, rhs, start=True, stop=True):
        return self.add("pe", lambda e: e.matmul(out, lhsT, rhs, start=start, stop=stop), [out], [lhsT, rhs])

    def lower(self, final_wait_ops=()):
        nc = self.nc
        fin = Op()
        fin.eng = "sp"
        fin.fn = None
        fin.dma = False
        fin.marked = False
        fin.idx = len(self.ops)
        fin.deps = set(o.idx for o in final_wait_ops)
        for d in fin.deps:
            self.ops[d].marked = True
        self.ops.append(fin)
        cnt = {e: 0 for e in ENGS}
        for op in self.ops:
            if not op.dma and op.marked:
                cnt[op.eng] += 1
        need = {e: max(1, (cnt[e] + EPOCH - 1) // EPOCH) for e in ENGS}
        with contextlib.ExitStack() as st:
            esems = {e: [st.enter_context(nc.semaphore("s_%s_%d" % (e, i))) for i in range(need[e])] for e in ENGS}
            dsems = [st.enter_context(nc.semaphore("s_dma_%d" % i)) for i in range(NDSEM)]
            c = {e: 0 for e in ENGS}
            dj = 0
            for op in self.ops:
                if op.dma:
                    op.sem = dsems[dj % NDSEM]
                    op.val = 16 * (dj // NDSEM + 1)
                    dj += 1
                elif op.marked:
                    k = c[op.eng]
                    c[op.eng] += 1
                    op.sem = esems[op.eng][k // EPOCH]
                    op.val = (k % EPOCH) + 1
            block = st.enter_context(nc.Block())
            per = {e: [op for op in self.ops if op.eng == e] for e in ENGS}
            ops = self.ops

            def emit(e, lst):
                waited = {}
                for op in lst:
                    ws = {}
                    for d in op.deps:
                        o = ops[d]
                        key = id(o.sem)
                        if key not in ws or ws[key][1] < o.val:
                            ws[key] = (o.sem, o.val)
                    for key, (sem, val) in ws.items():
                        if waited.get(key, 0) >= val:
                            continue
                        waited[key] = val
                        e.wait_ge(sem, val)
                    if op.fn is None:
                        continue
                    ins = op.fn(e)
                    if op.dma:
                        ins.then_inc(op.sem, 16)
                    elif op.marked:
                        ins.then_inc(op.sem, 1)

            @block.tensor
            def _(e):
                emit(e, per["pe"])

            @block.scalar
            def _(e):
                emit(e, per["act"])

            @block.vector
            def _(e):
                emit(e, per["dve"])

            @block.gpsimd
            def _(e):
                emit(e, per["pool"])

            @block.sync
            def _(e):
                emit(e, per["sp"])


D = 1024
KD = 8
GRID_W = 64
EPS = 1e-6
DFF = 2816
NFC = 22
EVEN_IN = 3600
ODD_IN = 5184
OFF_GQ, OFF_GK, OFF_GV, OFF_GG, OFF_MQ, OFF_MK, OFF_MV, OFF_MO, OFF_MG = 0, 256, 512, 1024, 1536, 2048, 2560, 3072, 3584
OFF_Z, OFF_X, OFF_B, OFF_C, OFF_DT = 0, 2048, 4096, 4608, 5120
WB = 1024
NEG = -1.0e30

DRAM_IN = [
    ("xT", [2, D, None]), ("cT", [128, KD, 3]),
    ("mod_w", [2, D, 6 * D]), ("modb", [128, 2, 48]), ("gmix", [128, 2, KD]), ("gffn", [128, 2, KD]), ("gfin", [128, KD]),
    ("w_up", [2, D, 2 * DFF]), ("cw", [128, 2, 44, 9]), ("cb", [128, 2, 44]), ("w_down", [2, DFF, D]),
    ("e_w_in", [D, EVEN_IN]), ("a1", [D, 32]), ("a2", [16, 2, 256]), ("ab", [128, 2, 2]),
    ("mlcw", [128, 8, 3]), ("mlcb", [128, 8]), ("gateb", [1, 16]), ("glang", [128, 4]), ("mlng", [128, 4]),
    ("e_w_out", [D, D]),
    ("s_w_in", [D, ODD_IN]), ("scw", [128, 24, 3]), ("scb", [128, 24]), ("sdtb", [4, 16]), ("salog", [4, 16]),
    ("sdrep", [1, 2048]), ("sng", [1, 2048]), ("s_w_out", [2048, D]),
    ("c_identf", [128, 128]), ("c_triu", [128, 128]), ("c_tril", [128, 128]),
]


class Builder:
    def __init__(self, TL, TC, RB, stop="full", nb=2):
        self.TL, self.TC, self.T = TL, TC, TL + TC
        self.NCL, self.NCC = TL // 128, TC // 128
        self.NCH = self.NCL + self.NCC
        self.ROWS = TL // GRID_W
        self.RB = min(RB, self.ROWS)
        self.stop = stop
        self.nb = nb
        self.nc = bass.Bass("TRN2", target_bir_lowering=False)
        self.S = Sched(self.nc)
        self.st = contextlib.ExitStack()
        self.d = {}
        for name, shp in DRAM_IN:
            shp = [self.T if s is None else s for s in shp]
            if name == "xT":
                shp[0] = nb
            self.d[name] = self.nc.dram_tensor(name, shp, F32, kind="ExternalInput").ap()
        self.outT = self.nc.dram_tensor("outT", [nb, D, TL], F32, kind="ExternalOutput").ap()
        self.psn = 0
        self.wi = 0
        self.out_ops = []

    def sb(self, name, shape, dt):
        return self.st.enter_context(self.nc.sbuf_tensor(name, shape, dt))

    def alloc(self):
        T = self.T
        self.XT = self.sb("XT", [128, KD, T], F32)
        self.hT = self.sb("hT", [128, KD, T], BF16)
        self.wst = self.sb("wst", [128, 2, WB], F32)
        self.wbf = self.sb("wbf", [128, 3, WB], BF16)
        self.PS = [self.st.enter_context(self.nc.psum_tensor("ps%d" % i, [128, 512], F32)) for i in range(8)]
        self.identf = self.sb("identf", [128, 128], F32)
        self.identb = self.sb("identb", [128, 128], BF16)
        self.triu = self.sb("triu", [128, 128], F32)
        self.tril = self.sb("tril", [128, 128], F32)
        self.onesf = self.sb("onesf", [128, 128], F32)
        self.onesb = self.sb("onesb", [128, 128], BF16)
        self.mskf = self.sb("mskf", [128, 128], BF16)
        self.mskb = self.sb("mskb", [128, 128], BF16)
        self.nmf = self.sb("nmf", [128, 128], BF16)
        self.nmb = self.sb("nmb", [128, 128], BF16)
        if "nomix" in KDBG:
            self.nmf32 = self.sb("nmf32", [128, 128], F32)
            self.nmb32 = self.sb("nmb32", [128, 128], F32)
        self.cTt = self.sb("cTt", [128, KD, 3], F32)
        self.scT = self.sb("scT", [128, KD, 3], F32)
        self.modT = self.sb("modT", [128, 2, 48, 3], F32)
        self.modb = self.sb("modbt", [128, 2, 48], F32)
        self.gmix = self.sb("gmixt", [128, 2, KD], F32)
        self.gffn = self.sb("gffnt", [128, 2, KD], F32)
        self.gfin = self.sb("gfint", [128, KD], F32)
        self.zero8 = self.sb("zero8", [128, KD], F32)
        self.Amix = self.sb("Amix", [128, 2, KD, 3], F32)
        self.Affn = self.sb("Affn", [128, 2, KD, 3], F32)
        self.cw = self.sb("cwt", [128, 2, 44, 9], BF16)
        self.cb = self.sb("cbt", [128, 2, 44], F32)
        self.a1b = self.sb("a1b", [128, KD, 32], BF16)
        self.nab = self.sb("nab", [128, 2, 2], F32)
        self.mlcw = self.sb("mlcwt", [128, 8, 3], F32)
        self.mlcb = self.sb("mlcbt", [128, 8], F32)
        self.gateb = self.sb("gatebt", [128, 16], F32)
        self.glang = self.sb("glangt", [128, 4], F32)
        self.mlng = self.sb("mlngt", [128, 4], F32)
        self.scw = self.sb("scwt", [128, 24, 3], F32)
        self.scb = self.sb("scbt", [128, 24], F32)
        self.SCRB = 25600 + 1152 + 640 + 512 + 1152
        self.SCRF = 4650
        self.scrb = self.sb("scrb", [128, self.SCRB], BF16)
        self.scrf = self.sb("scrf", [128, self.SCRF], F32)
        self.rsb = self.scrf[:, 0:1536].rearrange("p (i n) -> p i n", n=512)
        self.sqb = self.scrb[:, 0:1024].rearrange("p (i n) -> p i n", n=512)
        self.scr_yf = self.nc.dram_tensor("scr_yf", [self.NCL, 128, 512], BF16, kind="Internal").ap()

    def ps(self, i=None):
        if i is None:
            i = self.psn % 8
            self.psn += 1
        return self.PS[i]

    def act(self, out, in_, func, bias=0.0, scale=1.0, **kw):
        return self.S.call("act", "activation", out=out, in_=in_, func=func, bias=bias, scale=scale, **kw)

    def ts(self, out, in0, s1, s2=None, op0=ALU.mult, op1=None, eng="dve"):
        if op1 is None:
            if eng == "pool" and op0 == ALU.mult:
                return self.S.call(eng, "tensor_scalar", out=out, in0=in0, scalar1=s1, scalar2=0.0, op0=op0, op1=ALU.add)
            return self.S.call(eng, "tensor_scalar", out=out, in0=in0, scalar1=s1, scalar2=None, op0=op0)
        return self.S.call(eng, "tensor_scalar", out=out, in0=in0, scalar1=s1, scalar2=s2, op0=op0, op1=op1)

    def tt(self, out, in0, in1, op, eng="dve"):
        return self.S.call(eng, "tensor_tensor", out=out, in0=in0, in1=in1, op=op)

    def stt(self, out, in0, scalar, in1, op0, op1):
        return self.S.call("dve", "scalar_tensor_tensor", out=out, in0=in0, scalar=scalar, in1=in1, op0=op0, op1=op1)

    def cp(self, out, in_, eng="act"):
        if eng == "act":
            return self.act(out, in_, AF.Copy)
        return self.S.call(eng, "tensor_copy", out=out, in_=in_)

    def mm(self, *a, **k):
        return self.S.mm(*a, **k)

    def dma(self, out, in_, eng="sp"):
        return self.S.dma(eng, out, in_)

    def wload(self, src, K, C, eng="pool"):
        assert K * C <= WB
        i = self.wi
        self.wi += 1
        st = self.wst[:, i % 2, 0:K * C].rearrange("p (k c) -> p k c", c=C)
        bf = self.wbf[:, i % 3, 0:K * C].rearrange("p (k c) -> p k c", c=C)
        self.dma(st, src)
        self.S.call(eng, "tensor_copy", out=bf, in_=st)
        return bf

    def wview(self, w2d, c0, C, k0=0, K=KD):
        return w2d.rearrange("(k p) n -> p k n", p=128)[:, k0:k0 + K, c0:c0 + C]

    def tiles(self, lat=True, ctx=True):
        r = []
        if lat:
            for i in range(0, self.TL, 512):
                r.append((i, min(512, self.TL - i), False))
        if ctx:
            for i in range(0, self.TC, 512):
                r.append((self.TL + i, min(512, self.TC - i), True))
        return r

    def consts(self):
        d = self.d
        self.dma(self.identf[:], d["c_identf"])
        self.dma(self.triu[:], d["c_triu"])
        self.dma(self.tril[:], d["c_tril"])
        self.S.call("pool", "memset", ap=self.onesf[:], constant=1.0)
        self.S.call("pool", "memset", ap=self.onesb[:], constant=1.0)
        self.S.call("pool", "memset", ap=self.zero8[:], constant=0.0)
        self.cp(self.identb[:], self.identf[:], "dve")
        self.cp(self.mskf[:], self.triu[:], "dve")
        self.cp(self.mskb[:], self.tril[:], "dve")
        self.ts(self.nmf[:], self.triu[:], -1.0, -NEG, op0=ALU.add, op1=ALU.mult)
        self.ts(self.nmb[:], self.tril[:], -1.0, -NEG, op0=ALU.add, op1=ALU.mult)
        if "nomix" in KDBG:
            self.ts(self.nmf32[:], self.triu[:], -1.0, -NEG, op0=ALU.add, op1=ALU.mult)
            self.ts(self.nmb32[:], self.tril[:], -1.0, -NEG, op0=ALU.add, op1=ALU.mult)
        for nm, t in (("cT", self.cTt), ("modb", self.modb), ("gmix", self.gmix), ("gffn", self.gffn), ("gfin", self.gfin),
                      ("cb", self.cb), ("mlcw", self.mlcw), ("mlcb", self.mlcb),
                      ("glang", self.glang), ("mlng", self.mlng), ("scw", self.scw), ("scb", self.scb),
                      ):
            self.dma(t[:], d[nm])
        for l_ in range(2):
            cwst = self.wst[:, l_, 0:396].rearrange("p (c t) -> p c t", t=9)
            self.dma(cwst, d["cw"][:, l_])
            self.cp(self.cw[:, l_], cwst, "dve")
        self.dma(self.nab[:], d["ab"])
        self.ts(self.nab[:], self.nab[:], -1.0)

        self.dma(self.gateb[:], d["gateb"][0:1, :].to_broadcast([128, 16]))
        st = self.wst[:, 0, 0:256].rearrange("p (k c) -> p k c", c=32)
        self.dma(st, self.wview(d["a1"], 0, 32))
        self.cp(self.a1b[:], st, "pool")

    def mod_phase(self):
        d = self.d
        scb = self.scrb[:, 1024:1024 + KD * 3].rearrange("p (k s) -> p k s", s=3)
        self.act(scb, self.cTt[:], AF.Silu)
        for l in range(2):
            for jb in range(48):
                st = self.wload(self.wview(d["mod_w"][l], jb * 128, 128), KD, 128, eng=("pool" if jb % 2 else "dve"))
                ps = self.ps(jb % 2)
                for k in range(KD):
                    self.mm(ps[:, 0:3], st[:, k, :], scb[:, k, :], start=(k == 0), stop=(k == KD - 1))
                self.act(self.modT[:, l, jb, :], ps[:, 0:3], AF.Identity, bias=self.modb[:, l, jb:jb + 1])
            for (A, g, off) in ((self.Amix, self.gmix, 8), (self.Affn, self.gffn, 32)):
                self.ts(A[:, l], self.modT[:, l, off:off + 8, :], 1.0, op0=ALU.add)
                for s in range(3):
                    self.tt(A[:, l, :, s], A[:, l, :, s], g[:, l, :], ALU.mult)

    def norm(self, Afn, Bfn, lat=True, ctx=True, out_fp32=None):
        for ti, (t0, n, isc) in enumerate(self.tiles(lat, ctx)):
            ps = self.ps(ti % 2)
            for k in range(KD):
                sq = self.sqb[:, k % 2, 0:n]
                self.act(sq, self.XT[:, k, t0:t0 + n], AF.Square)
                self.mm(ps[:, 0:n], self.onesb[:], sq, start=(k == 0), stop=(k == KD - 1))
            rs = self.rsb[:, 0, 0:n]
            self.act(rs, ps[:, 0:n], AF.Ln, bias=self.epsD[:, 0:1], scale=1.0 / D)
            self.act(ps[:, 0:n], rs, AF.Exp, scale=-0.5)
            for k in range(KD):
                tmp = self.rsb[:, 1 + k % 2, 0:n] if out_fp32 is None else out_fp32(k, t0, n)
                self.tt(tmp, self.XT[:, k, t0:t0 + n], ps[:, 0:n], ALU.mult)
                if out_fp32 is None:
                    self.act(self.hT[:, k, t0:t0 + n], tmp, AF.Identity, bias=Bfn(k, isc), scale=Afn(k, isc))
                else:
                    self.ts(tmp, tmp, Afn(k, isc))

    def xt_update(self, ps_ap, f, t0, n, gate_ap):
        self.stt(self.XT[:, f, t0:t0 + n], ps_ap, gate_ap, self.XT[:, f, t0:t0 + n], ALU.mult, ALU.add)

    def ffn(self, l, s_lat, do_ctx):
        d = self.d
        TL, TC, RB = self.TL, self.TC, self.RB
        nblk = self.ROWS // RB
        NPAD = (RB + 2) * 66
        o = 0
        upad = []
        for i in range(2):
            upad.append(self.scrb[:, o:o + NPAD].rearrange("p (r w) -> p r w", w=66))
            o += NPAD
        upc = []
        for i in range(2):
            upc.append(self.scrb[:, o:o + TC + 2])
            o += TC + 2
        NA = RB * 64 + TC
        actb = self.scrb[:, o:o + 11 * NA].rearrange("p (c t) -> p c t", t=NA)
        o += 11 * NA
        sg = self.scrb[:, o:o + NA]
        o += NA
        dgs = []
        for i in range(2):
            dgs.append(self.scrb[:, o:o + 9 * 128].rearrange("p (a m) -> p a m", m=128))
            o += 9 * 128
        ident9 = self.scrb[:, o:o + 9 * 128].rearrange("p (a m) -> p a m", m=128)
        o += 9 * 128
        assert o <= self.SCRB, o
        for tap in range(9):
            self.cp(ident9[:, tap, :], self.identb[:], "pool")
        for i in range(2):
            self.S.call("pool", "memset", ap=upad[i], constant=0.0)
            self.S.call("pool", "memset", ap=upc[i], constant=0.0)
        g2 = lambda f, isc: self.modT[:, l, 40 + f, (2 if isc else s_lat):(3 if isc else s_lat + 1)]
        it = 0
        for blk in range(nblk):
            r0 = blk * RB
            ra, rb = max(r0 - 1, 0), min(r0 + RB + 1, self.ROWS)
            nr = rb - ra
            ntok = nr * 64
            urow0 = ra - (r0 - 1)
            with_ctx = do_ctx and blk == 0
            for half in range(2):
                items = [(cc, typ) for cc in range(11) for typ in ("g", "a")]

                def stageU(idx, it_):
                    cc, typ = items[idx]
                    cidx = half * 11 + cc + (NFC if typ == "g" else 0)
                    w = self.wload(self.wview(d["w_up"][l], cidx * 128, 128), KD, 128, eng="dve")
                    up = upad[it_ % 2]
                    uc = upc[it_ % 2]
                    dg = dgs[it_ % 2]
                    bset = (it_ % 2) * 3
                    self.tt(dg, ident9, self.cw[:, l, cidx, :].unsqueeze(2).to_broadcast([128, 9, 128]), ALU.mult, eng="pool")
                    if blk == 0:
                        self.S.call("pool", "memset", ap=up[:, 0:1, :], constant=0.0)
                    if blk == nblk - 1:
                        self.S.call("pool", "memset", ap=up[:, RB + 1:RB + 2, :], constant=0.0)
                    pieces = [(q, min(512, ntok - q)) for q in range(0, ntok, 512)]
                    for pi, (q, n) in enumerate(pieces):
                        ps = self.PS[bset + pi]
                        for k in range(KD):
                            self.mm(ps[:, 0:n], w[:, k, :], self.hT[:, k, ra * 64 + q:ra * 64 + q + n], start=(k == 0), stop=(k == KD - 1))
                        self.act(up[:, urow0 + q // 64:urow0 + (q + n) // 64, 1:65],
                                 ps[:, 0:n].rearrange("p (r w) -> p r w", w=64), AF.Copy)
                    if with_ctx:
                        psc = self.PS[bset + 2]
                        assert TC <= 256 and ntok - 1024 <= 256
                        for k in range(KD):
                            self.mm(psc[:, 256:256 + TC], w[:, k, :], self.hT[:, k, TL:TL + TC], start=(k == 0), stop=(k == KD - 1))
                        self.act(uc[:, 1:1 + TC], psc[:, 256:256 + TC], AF.Copy)

                def stageC(idx, it_):
                    cc, typ = items[idx]
                    cidx = half * 11 + cc + (NFC if typ == "g" else 0)
                    up = upad[it_ % 2]
                    uc = upc[it_ % 2]
                    dg = dgs[it_ % 2]
                    for q in range(0, RB * 64, 512):
                        n = min(512, RB * 64 - q)
                        nrow = n // 64
                        i0 = q // 64
                        pc = self.PS[6 + (self.psn % 2)]
                        self.psn += 1
                        for tap in range(9):
                            dr, dw = tap // 3, tap % 3
                            self.mm(pc[:, 0:n].rearrange("p (r w) -> p r w", w=64), dg[:, tap, :],
                                    up[:, i0 + dr:i0 + dr + nrow, dw:dw + 64], start=(tap == 0), stop=(tap == 8))
                        self._ffn_evac(typ, pc[:, 0:n], q, n, cc, cidx, l, actb, sg)
                    if with_ctx:
                        pc = self.PS[6 + (self.psn % 2)]
                        self.psn += 1
                        for dw in range(3):
                            self.mm(pc[:, 0:TC], dg[:, 3 + dw, :], uc[:, dw:dw + TC], start=(dw == 0), stop=(dw == 2))
                        self._ffn_evac(typ, pc[:, 0:TC], RB * 64, TC, cc, cidx, l, actb, sg)

                stageU(0, it)
                for idx in range(len(items)):
                    if idx + 1 < len(items):
                        stageU(idx + 1, it + idx + 1)
                    stageC(idx, it + idx)
                it += len(items)
                for f in range(KD):
                    w1 = self.wload(self.wview(d["w_down"][l], f * 128, 128, k0=half * 11, K=8), 8, 128, eng="dve")
                    w2 = self.wload(self.wview(d["w_down"][l], f * 128, 128, k0=half * 11 + 8, K=3), 3, 128, eng="dve")
                    segs = [(q, min(512, RB * 64 - q), False) for q in range(0, RB * 64, 512)]
                    if with_ctx:
                        segs.append((RB * 64, TC, True))
                    for (q, n, isc) in segs:
                        ps = self.PS[self.psn % 6]
                        self.psn += 1
                        for c2 in range(11):
                            wk = w1[:, c2, :] if c2 < 8 else w2[:, c2 - 8, :]
                            self.mm(ps[:, 0:n], wk, actb[:, c2, q:q + n], start=(c2 == 0), stop=(c2 == 10))
                        t0 = (TL + 0) if isc else (r0 * 64 + q)
                        self.xt_update(ps[:, 0:n], f, t0, n, g2(f, isc))

    def _ffn_evac(self, typ, pc, q, n, cc, cidx, l, actb, sg):
        if typ == "g":
            self.act(sg[:, q:q + n], pc, AF.Silu, bias=self.cb[:, l, cidx:cidx + 1])
        else:
            self.stt(actb[:, cc, q:q + n], pc, self.cb[:, l, cidx:cidx + 1], sg[:, q:q + n], ALU.add, ALU.mult)

    def chunk_order(self, d):
        lat = list(range(self.NCL))
        ctx = list(range(self.NCL, self.NCH))
        return (ctx + lat) if d == 0 else (ctx[::-1] + lat[::-1])

    def conv1d(self, w_cols, cpad, acc, wv, bv, dests, func, post_scale=None):
        TL, TC = self.TL, self.TC
        for ti, (t0, n, isc) in enumerate(self.tiles()):
            ps = self.PS[6 + ti % 2]
            for k in range(KD):
                self.mm(ps[:, 0:n], w_cols[:, k, :], self.hT[:, k, t0:t0 + n], start=(k == 0), stop=(k == KD - 1))
            c0 = (t0 + 1) if not isc else (t0 + 3)
            self.act(cpad[:, c0:c0 + n], ps[:, 0:n], AF.Copy)
        for ti, (t0, n, isc) in enumerate(self.tiles()):
            c0 = (t0 + 1) if not isc else (t0 + 3)
            a = acc[:, 0, 0:n]
            self.ts(a, cpad[:, c0 - 1:c0 - 1 + n], wv[:, 0:1], bv, op0=ALU.mult, op1=ALU.add)
            self.stt(a, cpad[:, c0:c0 + n], wv[:, 1:2], a, ALU.mult, ALU.add)
            self.stt(a, cpad[:, c0 + 1:c0 + 1 + n], wv[:, 2:3], a, ALU.mult, ALU.add)
            self.act(dests(t0, n), a, func)

    def zero_cpad(self, cpad):
        TL, TC = self.TL, self.TC
        self.S.call("pool", "memset", ap=cpad[:, 0:1], constant=0.0)
        self.S.call("pool", "memset", ap=cpad[:, TL + 1:TL + 3], constant=0.0)
        self.S.call("pool", "memset", ap=cpad[:, TL + TC + 3:TL + TC + 4], constant=0.0)

    def even_mixer(self, l, s_lat):
        d = self.d
        T, TL, TC, NCH = self.T, self.TL, self.TC, self.NCH
        W = d["e_w_in"]
        g1 = lambda f, isc: self.modT[:, l, 16 + f, (2 if isc else s_lat):(3 if isc else s_lat + 1)]
        samp = lambda c: c >= self.NCL
        ob = [0]

        def cb_(n):
            a = self.scrb[:, ob[0]:ob[0] + n]
            ob[0] += n
            return a
        of_ = [0]

        def cf_(n):
            a = self.scrf[:, of_[0]:of_[0] + n]
            of_[0] += n
            return a
        qT = cb_(T)
        kT = cb_(T)
        sgT = cb_(2 * T).rearrange("p (j t) -> p j t", t=T)
        vtok = cb_(NCH * 256).rearrange("p (c v) -> p c v", v=256)
        ofw = cb_(NCH * 256).rearrange("p (c v) -> p c v", v=256)
        OT = cb_(2 * T).rearrange("p (j t) -> p j t", t=T)
        rsbb = cb_(128)
        a2l = cb_(256).rearrange("p (d c) -> p d c", c=128)
        qts = [cb_(128) for _ in range(2)]
        kt = cb_(128)
        ktoks = [cb_(128) for _ in range(2)]
        ktok = ktoks[0]
        ATms = [cb_(256).rearrange("p (j t) -> p j t", t=128) for _ in range(2)]
        onbs = [cb_(256).rearrange("p (j t) -> p j t", t=128) for _ in range(2)]
        Sb = cb_(128)
        vaugs = [cb_(130) for _ in range(2)]
        Cb = cb_(130)
        STms = [cb_(128) for _ in range(2)]
        hns = [cb_(128) for _ in range(2)]
        assert ob[0] <= self.SCRB, ob[0]
        gla0 = of_[0]
        e1 = cf_(128)
        sp = cf_(128)
        Ss = cf_(128)
        u = cf_(128)
        EQ = cf_(128)
        EK = cf_(128)
        S32 = cf_(128)
        tmpS = cf_(128)
        osum = cf_(256).rearrange("p (j t) -> p j t", t=128)
        sq2 = cf_(256).rearrange("p (j t) -> p j t", t=128)
        glaend = of_[0]
        of_[0] = gla0
        cpad = cf_(T + 4)
        of_[0] = max(of_[0], glaend)
        ssq = cf_(2)
        decs = [cf_(1) for _ in range(2)]
        C32 = cf_(130)
        gt = cf_(NCH * 16).rearrange("p (c g) -> p c g", g=16)
        spg = cf_(NCH * 8).rearrange("p (c g) -> p c g", g=8)
        cums = cf_(NCH * 16).rearrange("p (c g) -> p c g", g=16)
        wts = cf_(NCH * 8).rearrange("p (c g) -> p c g", g=8)
        ebs = cf_(NCH * 8).rearrange("p (c g) -> p c g", g=8)
        ets = cf_(NCH * 8).rearrange("p (c g) -> p c g", g=8)
        den = cf_(1)
        fac = cf_(1)
        hb = cf_(128)
        hss = [cf_(128) for _ in range(2)]
        tmpC = cf_(130)
        ssq1 = cf_(1)
        acc = cf_(512).rearrange("p (i n) -> p i n", n=512)
        assert of_[0] <= self.SCRF, of_[0]
        PS = self.PS

        for hp in range(2):
            for (dst, c0, fn) in ((qT, OFF_GQ + hp * 128, AF.Copy), (kT, OFF_GK + hp * 128, AF.Copy),
                                  (sgT[:, 0, :], OFF_GG + hp * 256, AF.Silu), (sgT[:, 1, :], OFF_GG + hp * 256 + 128, AF.Silu)):
                w = self.wload(self.wview(W, c0, 128), KD, 128)
                for ti, (t0, n, isc) in enumerate(self.tiles()):
                    ps = PS[6 + ti % 2]
                    for k in range(KD):
                        self.mm(ps[:, 0:n], w[:, k, :], self.hT[:, k, t0:t0 + n], start=(k == 0), stop=(k == KD - 1))
                    self.act(dst[:, t0:t0 + n], ps[:, 0:n], fn)
            for j in range(2):
                w = self.wload(self.wview(W, OFF_GV + hp * 256 + j * 128, 128), KD, 128)
                for c in range(NCH):
                    ps = PS[6 + c % 2]
                    for k in range(KD):
                        self.mm(ps[:, 0:128], self.hT[:, k, c * 128:(c + 1) * 128], w[:, k, :], start=(k == 0), stop=(k == KD - 1))
                    self.cp(vtok[:, c, j * 128:(j + 1) * 128], ps[:, 0:128])
            PSA = (PS[3], PS[7])
            PSO = (PS[4], PS[6])
            a2t = []
            for dr_ in range(2):
                i_ = self.wi
                self.wi += 1
                st_ = self.wst[0:16, i_ % 2, 0:128]
                bf_ = self.wbf[0:16, i_ % 3, 0:128]
                self.dma(st_, d["a2"][:, dr_, hp * 128:(hp + 1) * 128])
                self.cp(a2l[0:16, dr_, :], st_, "dve")

            def glaA(dr, c, sl):
                msk = self.mskf if dr == 0 else self.mskb
                tk = slice(c * 128, (c + 1) * 128)
                for k in range(KD):
                    self.mm(PS[0][0:16, 0:128], self.a1b[:, k, dr * 16:(dr + 1) * 16], self.hT[:, k, tk], start=(k == 0), stop=(k == KD - 1))
                self.cp(rsbb[0:16, :], PS[0][0:16, 0:128])
                self.mm(PS[0][:, 128:256], a2l[0:16, dr, :], rsbb[0:16, :])
                self.act(e1, PS[0][:, 128:256], AF.Exp, bias=self.nab[:, dr, hp:hp + 1], scale=-1.0)
                self.act(sp, e1, AF.Ln, bias=1.0)
                self.S.call("dve", "tensor_tensor_scan", out=Ss, data0=self.onesf[:], data1=sp, initial=0.0, op0=ALU.mult, op1=ALU.add)
                if dr == 0:
                    uu = Ss
                else:
                    self.ts(u, Ss, -1.0, Ss[:, 127:128], op0=ALU.mult, op1=ALU.add)
                    self.tt(u, u, sp, ALU.add, eng="pool")
                    uu = u
                self.act(EQ, uu, AF.Exp, bias=self.lnq[:, 0:1], scale=-1.0 / 16.0)
                self.act(EK, uu, AF.Exp, scale=1.0 / 16.0)
                self.act(decs[sl], Ss[:, 127:128], AF.Exp, scale=-1.0 / 16.0)
                self.tt(qts[sl], qT[:, tk], EQ, ALU.mult)
                self.tt(kt, kT[:, tk], EK, ALU.mult, eng="pool")
                self.mm(PS[2][:, 0:128], kt, self.identb[:])
                self.cp(ktoks[sl], PS[2][:, 0:128])
                for j in range(2):
                    hs_ = slice(j * 64, (j + 1) * 64)
                    self.mm(PSA[j][:, 0:128], kt[hs_, :], qts[sl][hs_, :])
                for j in range(2):
                    self.tt(ATms[sl][:, j, :], PSA[j][:, 0:128], msk[:], ALU.mult)

            def glaB(dr, ci, c, sl):
                tk = slice(c * 128, (c + 1) * 128)
                qt_, ktok_, ATm_, dec_ = qts[sl], ktoks[sl], ATms[sl], decs[sl]
                for j in range(2):
                    hs_ = slice(j * 64, (j + 1) * 64)
                    self.mm(PSO[j][:, 0:128], ATm_[:, j, :], vtok[:, c, j * 128:(j + 1) * 128], start=True, stop=False)
                    self.mm(PSO[j][:, 0:128], qt_[hs_, :], Sb[hs_, :], start=False, stop=True)
                if ci < NCH - 1:
                    for j in range(2):
                        self.mm(PS[1][j * 64:(j + 1) * 64, 0:128], ktok_[:, j * 64:(j + 1) * 64], vtok[:, c, j * 128:(j + 1) * 128])
                    self.tt(tmpS, S32, PS[1][:, 0:128], ALU.add)
                    self.act(Sb, tmpS, AF.Copy, scale=dec_)
                    self.ts(S32, tmpS, dec_)
                if dr == 0:
                    for j in range(2):
                        self.cp(ofw[:, c, j * 128:(j + 1) * 128], PSO[j][:, 0:128])
                else:
                    for j in range(2):
                        self.tt(osum[:, j, :], PSO[j][:, 0:128], ofw[:, c, j * 128:(j + 1) * 128], ALU.add)
                    self.tt(sq2, osum, osum, ALU.mult, eng="pool")
                    self.S.call("dve", "tensor_reduce", out=ssq, in_=sq2, axis=AX.X, op=ALU.add)
                    self.act(ssq, ssq, AF.Ln, bias=self.epsD[:, 0:1], scale=1.0 / 128.0)
                    self.act(ssq, ssq, AF.Exp, scale=-0.5)
                    self.tt(onbs[sl], osum, ssq.unsqueeze(2).to_broadcast([128, 2, 128]), ALU.mult)

            def glaT(c, sl):
                tk = slice(c * 128, (c + 1) * 128)
                for j in range(2):
                    self.mm(PS[5][:, j * 128:(j + 1) * 128], onbs[sl][:, j, :], self.identb[:])
                for j in range(2):
                    self.stt(OT[:, j, tk], PS[5][:, j * 128:(j + 1) * 128], self.glang[:, hp * 2 + j:hp * 2 + j + 1], sgT[:, j, tk], ALU.mult, ALU.mult)

            for dr in range(2):
                seq = self.chunk_order(dr)
                self.S.call("pool", "memset", ap=S32, constant=0.0)
                self.S.call("pool", "memset", ap=Sb, constant=0.0)
                glaA(dr, seq[0], 0)
                for ci, c in enumerate(seq):
                    if ci + 1 < len(seq):
                        glaA(dr, seq[ci + 1], (ci + 1) % 2)
                    glaB(dr, ci, c, ci % 2)
                    if dr == 1 and ci >= 1:
                        glaT(seq[ci - 1], (ci - 1) % 2)
                if dr == 1:
                    glaT(seq[-1], (len(seq) - 1) % 2)
            for f in range(KD):
                w = self.wload(self.wview(d["e_w_out"], f * 128, 128, k0=hp * 2, K=2), 2, 128)
                for ti, (t0, n, isc) in enumerate(self.tiles()):
                    ps = PS[6 + ti % 2]
                    for j in range(2):
                        self.mm(ps[:, 0:n], w[:, j, :], OT[:, j, t0:t0 + n], start=(j == 0), stop=(j == 1))
                    self.xt_update(ps[:, 0:n], f, t0, n, g1(f, isc))

        w = self.wload(self.wview(W, OFF_MG, 16), KD, 16)
        for c in range(NCH):
            ps = PS[c % 2]
            for k in range(KD):
                self.mm(ps[:, 0:16], self.hT[:, k, c * 128:(c + 1) * 128], w[:, k, :], start=(k == 0), stop=(k == KD - 1))
            self.tt(gt[:, c, :], ps[:, 0:16], self.gateb[:], ALU.add)
        for dr in range(2):
            self.act(spg[:, :, dr * 4:(dr + 1) * 4], gt[:, :, dr * 8 + 4:dr * 8 + 8], AF.Exp, scale=-1.0)
        self.act(spg, spg, AF.Ln, bias=1.0)
        assert NCH * 16 <= 512
        for c in range(NCH):
            for dr in range(2):
                tri = self.triu if dr == 0 else self.tril
                self.mm(PS[2][:, c * 16 + dr * 8:c * 16 + dr * 8 + 4], tri[:], spg[:, c, dr * 4:(dr + 1) * 4])
                self.mm(PS[2][:, c * 16 + dr * 8 + 4:c * 16 + dr * 8 + 8], self.onesf[:], spg[:, c, dr * 4:(dr + 1) * 4])
        self.ts(cums.rearrange("p c g -> p (c g)"), PS[2][:, 0:NCH * 16], -1.0)
        cv = cums.rearrange("p c (d g) -> p c d g", d=2)
        gv = gt.rearrange("p c (d g) -> p c d g", d=2)
        wv_ = wts.rearrange("p c (d g) -> p c d g", d=2)
        ev_ = ebs.rearrange("p c (d g) -> p c d g", d=2)
        tv_ = ets.rearrange("p c (d g) -> p c d g", d=2)
        for dr in range(2):
            self.tt(wv_[:, :, dr, :], gv[:, :, dr, 0:4], cv[:, :, dr, 0:4], ALU.subtract)
            self.act(wv_[:, :, dr, :], wv_[:, :, dr, :], AF.Exp, bias=self.lnk[:, 0:1])
            self.act(ev_[:, :, dr, :], cv[:, :, dr, 0:4], AF.Exp)
            self.act(tv_[:, :, dr, :], cv[:, :, dr, 4:8], AF.Exp)
        self.zero_cpad(cpad)
        for m in range(4):
            w = self.wload(self.wview(W, OFF_MQ + m * 128, 128), KD, 128)
            self.conv1d(w, cpad, acc, self.mlcw[:, m, :], self.mlcb[:, m:m + 1], lambda t0, n: qT[:, t0:t0 + n], AF.Silu)
            w = self.wload(self.wview(W, OFF_MK + m * 128, 128), KD, 128)
            self.conv1d(w, cpad, acc, self.mlcw[:, 4 + m, :], self.mlcb[:, 4 + m:5 + m], lambda t0, n: kT[:, t0:t0 + n], AF.Silu)
            vt = vtok.rearrange("p c (j v) -> p c j v", j=2)[:, :, 0, :]
            smo = vtok.rearrange("p c (j v) -> p c j v", j=2)[:, :, 1, :]
            hf = ofw.rearrange("p c (j v) -> p c j v", j=2)[:, :, 0, :]
            for (dst, c0, fn) in ((vt, OFF_MV + m * 128, AF.Copy), (smo, OFF_MO + m * 128, AF.Sigmoid)):
                w = self.wload(self.wview(W, c0, 128), KD, 128)
                for c in range(NCH):
                    ps = PS[6 + c % 2]
                    for k in range(KD):
                        self.mm(ps[:, 0:128], self.hT[:, k, c * 128:(c + 1) * 128], w[:, k, :], start=(k == 0), stop=(k == KD - 1))
                    self.act(dst[:, c, :], ps[:, 0:128], fn)
            OTm = OT[:, 0, :]
            def mlA(dr, c, sl):
                msk = self.mskf if dr == 0 else self.mskb
                tk = slice(c * 128, (c + 1) * 128)
                wcol = wv_[:, c, dr, m:m + 1]
                self.mm(PS[3][:, 0:128], kT[:, tk], qT[:, tk])
                self.tt(STms[sl], PS[3][:, 0:128], msk[:], ALU.mult)
                self.ts(vaugs[sl][:, 0:128], vt[:, c, :], wcol, eng="pool")
                self.cp(vaugs[sl][:, 128:129], wcol, "pool")
                self.mm(PS[2][:, 0:128], kT[:, tk], self.identb[:])
                self.cp(ktoks[sl], PS[2][:, 0:128])

            def mlB(dr, ci, c, sl):
                tk = slice(c * 128, (c + 1) * 128)
                ecol = ev_[:, c, dr, m:m + 1]
                tcol = tv_[:, c, dr, m:m + 1]
                vaug_, STm_, ktok_ = vaugs[sl], STms[sl], ktoks[sl]
                PSN = PS[4] if sl == 0 else PS[6]
                hs = hss[sl]
                hn = hns[sl]
                self.mm(PSN[:, 0:129], STm_, vaug_[:, 0:129], start=True, stop=False)
                self.mm(PSN[:, 0:129], qT[:, tk], Cb[:, 0:129], start=False, stop=True)
                if ci < NCH - 1:
                    self.mm(PS[7][:, 0:129], ktok_, vaug_[:, 0:129])
                    self.tt(tmpC[:, 0:129], C32[:, 0:129], PS[7][:, 0:129], ALU.add)
                    self.act(Cb[:, 0:129], tmpC[:, 0:129], AF.Copy, scale=tcol)
                    self.ts(C32[:, 0:129], tmpC[:, 0:129], tcol)
                self.act(den, PSN[:, 128:129], AF.Abs, scale=ecol)
                self.ts(den, den, 1.0, op0=ALU.max)
                self.S.call("dve", "reciprocal", out=den, in_=den)
                self.tt(fac, den, ecol, ALU.mult)
                if dr == 0:
                    self.ts(hf[:, c, :], PSN[:, 0:128], fac)
                else:
                    self.stt(hs, PSN[:, 0:128], fac, hf[:, c, :], ALU.mult, ALU.add)
                    self.tt(hs, hs, smo[:, c, :], ALU.mult, eng="pool")
                    self.act(hb, hs, AF.Square, accum_out=ssq1)
                    self.act(ssq1, ssq1, AF.Ln, bias=self.epsD[:, 0:1], scale=1.0 / 128.0)
                    self.act(ssq1, ssq1, AF.Exp, scale=-0.5)
                    self.ts(hn, hs, ssq1)

            def mlT(c, sl):
                tk = slice(c * 128, (c + 1) * 128)
                self.mm(PS[5][:, 0:128], hns[sl], self.identb[:])
                self.ts(OTm[:, tk], PS[5][:, 0:128], self.mlng[:, m:m + 1])

            for dr in range(2):
                seq = self.chunk_order(dr)
                self.S.call("pool", "memset", ap=C32, constant=0.0)
                self.S.call("pool", "memset", ap=Cb, constant=0.0)
                mlA(dr, seq[0], 0)
                for ci, c in enumerate(seq):
                    if ci + 1 < len(seq):
                        mlA(dr, seq[ci + 1], (ci + 1) % 2)
                    mlB(dr, ci, c, ci % 2)
                    if dr == 1 and ci >= 1:
                        mlT(seq[ci - 1], (ci - 1) % 2)
                if dr == 1:
                    mlT(seq[-1], (len(seq) - 1) % 2)
            for f in range(KD):
                w = self.wload(self.wview(d["e_w_out"], f * 128, 128, k0=4 + m, K=1), 1, 128)
                for ti, (t0, n, isc) in enumerate(self.tiles()):
                    ps = PS[6 + ti % 2]
                    self.mm(ps[:, 0:n], w[:, 0, :], OTm[:, t0:t0 + n])
                    self.xt_update(ps[:, 0:n], f, t0, n, g1(f, isc))

    def ssd_mixer(self, l, s_lat):
        d = self.d
        T, TL, TC, NCH, NCL = self.T, self.TL, self.TC, self.NCH, self.NCL
        W = d["s_w_in"]
        g1 = lambda f: self.modT[:, l, 16 + f, s_lat:s_lat + 1]
        PS = self.PS
        ob = [0]

        def cb_(n):
            a = self.scrb[:, ob[0]:ob[0] + n]
            ob[0] += n
            return a
        of_ = [0]

        def cf_(n):
            a = self.scrf[:, of_[0]:of_[0] + n]
            of_[0] += n
            return a
        xtok = cb_(NCH * 512).rearrange("p (c v) -> p c v", v=512)
        BT = cb_(T)
        CT = cb_(T)
        wz = cb_(KD * 512).rearrange("p (k c) -> p k c", c=512)
        Ebuf = cb_(1024).rearrange("p (h t) -> p h t", t=128)
        Wt = [cb_(1024).rearrange("p (h t) -> p h t", t=128) for _ in range(2)]
        CBTb = [cb_(128) for _ in range(2)]
        Btks = [cb_(128) for _ in range(2)]
        xhs = [cb_(512) for _ in range(2)]
        Hb = cb_(512)
        szb = cb_(512)
        ynbs = [cb_(512) for _ in range(2)]
        yfs = [cb_(512) for _ in range(2)]
        ahi = cb_(NCH * 16).rearrange("p (c g) -> p c g", g=16)
        alo = cb_(NCH * 16).rearrange("p (c g) -> p c g", g=16)
        ovl0 = ob[0]
        ygT = cb_(4 * 512).rearrange("p (c v) -> p c v", v=512)
        ob[0] = ovl0
        xTc = [cb_(T)] * 2
        ob[0] = max(ob[0], ovl0 + 2048)
        assert ob[0] <= self.SCRB, ob[0]
        NG = 16
        dta = cf_(NCH * NG).rearrange("p (c g) -> p c g", g=NG)
        bia = cf_(NCH * NG).rearrange("p (c g) -> p c g", g=NG)
        ecu = cf_(NCH * NG).rearrange("p (c g) -> p c g", g=NG)
        tmpA = cf_(NCH * NG).rearrange("p (c g) -> p c g", g=NG)
        tmpB = cf_(NCH * NG).rearrange("p (c g) -> p c g", g=NG)
        wen = tmpA
        eto = tmpB
        Ab = cf_(NG)
        dtb = cf_(NG)
        ssq = cf_(1)
        ov = of_[0]
        H32 = cf_(512)
        ytmp = cf_(512)
        ysum = cf_(512)
        ydx = cf_(512)
        Dg = cf_(512)
        ngt = cf_(512)
        e1 = of_[0]
        of_[0] = ov
        acc = cf_(512).rearrange("p (i n) -> p i n", n=512)
        cpad = cf_(T + 4)
        of_[0] = max(of_[0], e1)
        assert of_[0] <= self.SCRF, of_[0]
        fl = lambda a: a.rearrange("p c g -> p (c g)")
        for g in range(4):
            if KLVL <= 0:
                return
            self.zero_cpad(cpad)
            self.dma(Ab, d["salog"][g:g + 1, :].to_broadcast([128, NG]))
            self.dma(dtb, d["sdtb"][g:g + 1, :].to_broadcast([128, NG]))
            self.act(Ab, Ab, AF.Exp)
            self.ts(Ab, Ab, -1.0)
            if KLVL <= 1:
                continue
            wd = []
            for dr in range(2):
                wd.append(self.wload(self.wview(W, OFF_DT + dr * 32 + g * 8, 8), KD, 8))
            for c in range(NCH):
                ps = PS[c % 2]
                for dr in range(2):
                    for k in range(KD):
                        self.mm(ps[:, dr * 8:(dr + 1) * 8], self.hT[:, k, c * 128:(c + 1) * 128], wd[dr][:, k, :], start=(k == 0), stop=(k == KD - 1))
                self.tt(tmpA[:, c, :], ps[:, 0:NG], dtb, ALU.add)
            if KLVL <= 2:
                continue
            self.act(fl(tmpA), fl(tmpA), AF.Exp)
            self.act(fl(tmpA), fl(tmpA), AF.Ln, bias=1.0)
            self.act(fl(tmpB), fl(tmpA), AF.Ln)
            self.tt(dta.rearrange("p c g -> p g c"), tmpA.rearrange("p c g -> p g c"), Ab.unsqueeze(2).to_broadcast([128, NG, NCH]), ALU.mult)
            self.cp(ahi, dta, "dve")
            self.tt(alo, dta, ahi, ALU.subtract)
            if KLVL <= 3:
                continue
            assert NCH * 32 <= 1024
            for c in range(NCH):
                pb = PS[2 + (c * 32) // 512]
                o = (c * 32) % 512
                self.mm(pb[:, o:o + 8], self.triu[:], dta[:, c, 0:8])
                self.mm(pb[:, o + 8:o + 16], self.tril[:], dta[:, c, 8:16])
                self.mm(pb[:, o + 16:o + 32], self.onesf[:], dta[:, c, :])
            if KLVL <= 4:
                continue
            for c0 in range(0, NCH, 16):
                cn = min(16, NCH - c0)
                pb = PS[2 + c0 // 16]
                pv = pb[:, 0:cn * 32].rearrange("p (c g) -> p c g", g=32)
                self.tt(bia[:, c0:c0 + cn, :], tmpB[:, c0:c0 + cn, :], pv[:, :, 0:16], ALU.subtract)
                self.act(ecu[:, c0:c0 + cn, :], pv[:, :, 0:16], AF.Exp)
                self.tt(wen[:, c0:c0 + cn, :], bia[:, c0:c0 + cn, :], pv[:, :, 16:32], ALU.add)
                self.act(eto[:, c0:c0 + cn, :], pv[:, :, 16:32], AF.Exp)
            self.act(fl(wen), fl(wen), AF.Exp)
            if "ssd1" in KDBG:
                continue
            for (dst, c0, ci_) in ((BT, OFF_B + g * 128, 16 + g), (CT, OFF_C + g * 128, 20 + g)):
                w = self.wload(self.wview(W, c0, 128), KD, 128)
                self.conv1d(w, cpad, acc, self.scw[:, ci_, :], self.scb[:, ci_:ci_ + 1], lambda t0, n, dst=dst: dst[:, t0:t0 + n], AF.Silu)
            for j in range(4):
                w = self.wload(self.wview(W, OFF_X + g * 512 + j * 128, 128), KD, 128)
                xc = xTc[j % 2]
                self.conv1d(w, cpad, acc, self.scw[:, g * 4 + j, :], self.scb[:, g * 4 + j:g * 4 + j + 1], lambda t0, n, xc=xc: xc[:, t0:t0 + n], AF.Silu)
                for c in range(NCH):
                    ps = PS[6 + c % 2]
                    self.mm(ps[:, 0:128], xc[:, c * 128:(c + 1) * 128], self.identb[:])
                    self.cp(xtok[:, c, j * 128:(j + 1) * 128], ps[:, 0:128], "dve" if c % 2 else "act")
            for j in range(4):
                w = self.wload(self.wview(W, OFF_Z + g * 512 + j * 128, 128), KD, 128)
                self.cp(wz[:, :, j * 128:(j + 1) * 128], w, "pool")
            self.dma(Dg, d["sdrep"][0:1, g * 512:(g + 1) * 512].to_broadcast([128, 512]))
            self.dma(ngt, d["sng"][0:1, g * 512:(g + 1) * 512].to_broadcast([128, 512]))
            yi = [0]

            def stageA(dr, c, slot):
                trib = self.mskf if dr == 0 else self.mskb
                nm = self.nmf if dr == 0 else self.nmb
                tk = slice(c * 128, (c + 1) * 128)
                self.mm(PS[0][:, 0:128], BT[:, tk], CT[:, tk])
                self.cp(CBTb[slot], PS[0][:, 0:128])
                for h in range(8):
                    pb = PS[2 + h // 4]
                    oo = (h % 4) * 128
                    self.mm(pb[:, oo:oo + 128], ahi[:, c, dr * 8 + h:dr * 8 + h + 1].to_broadcast([128, 128]), trib[:], start=True, stop=False)
                    self.mm(pb[:, oo:oo + 128], alo[:, c, dr * 8 + h:dr * 8 + h + 1].to_broadcast([128, 128]), trib[:], start=False, stop=False)
                    self.mm(pb[:, oo:oo + 128], self.identb[:], nm[:], start=False, stop=True)
                for hb_ in range(2):
                    pv = PS[2 + hb_][:, 0:512].rearrange("p (h t) -> p h t", t=128)
                    self.tt(pv, pv, bia[:, c, dr * 8 + hb_ * 4:dr * 8 + hb_ * 4 + 4].unsqueeze(2).to_broadcast([128, 4, 128]), ALU.add)
                    self.act(Ebuf[:, hb_ * 4:hb_ * 4 + 4, :], pv, AF.Exp)
                for h in range(8):
                    self.tt(Wt[slot][:, h, :], Ebuf[:, h, :], CBTb[slot], ALU.mult, eng=("pool" if h >= 6 else "dve"))

            def stageS(dr, c, sl):
                tk = slice(c * 128, (c + 1) * 128)
                self.tt(xhs[sl].rearrange("p (h q) -> p h q", q=64), xtok[:, c, :].rearrange("p (h q) -> p h q", q=64),
                        wen[:, c, dr * 8:(dr + 1) * 8].unsqueeze(2).to_broadcast([128, 8, 64]), ALU.mult, eng="pool")
                self.mm(PS[0][:, 128:256], BT[:, tk], self.identb[:])
                self.cp(Btks[sl], PS[0][:, 128:256])

            def stageB(dr, ci, c, slot, sl):
                tk = slice(c * 128, (c + 1) * 128)
                is_ctx = c >= NCL
                upd = ci < NCH - 1
                if upd:
                    self.tt(H32.rearrange("p (h q) -> p h q", q=64), H32.rearrange("p (h q) -> p h q", q=64),
                            eto[:, c, dr * 8:(dr + 1) * 8].unsqueeze(2).to_broadcast([128, 8, 64]), ALU.mult, eng="pool")
                if not is_ctx:
                    self.mm(PS[5][:, 0:512], CT[:, tk], Hb)
                    if dr == 1:
                        for k in range(KD):
                            self.mm(PS[1][:, 0:512], self.hT[:, k, tk], wz[:, k, :], start=(k == 0), stop=(k == KD - 1))
                        self.act(szb, PS[1][:, 0:512], AF.Silu)
                if upd:
                    self.mm(PS[7][:, 0:512], Btks[sl], xhs[sl])
                    self.tt(H32, H32, PS[7][:, 0:512], ALU.add)
                    self.cp(Hb, H32, "pool")
                if not is_ctx:
                    wt = Wt[slot]
                    for h in range(8):
                        self.mm(PS[4][:, h * 64:(h + 1) * 64], wt[:, h, :], xtok[:, c, h * 64:(h + 1) * 64])
                    self.tt(ytmp.rearrange("p (h q) -> p h q", q=64), PS[5][:, 0:512].rearrange("p (h q) -> p h q", q=64),
                            ecu[:, c, dr * 8:(dr + 1) * 8].unsqueeze(2).to_broadcast([128, 8, 64]), ALU.mult)
                    yf = yfs[yi[0] % 2]
                    yi[0] += 1
                    if dr == 0:
                        self.tt(yf, PS[4][:, 0:512], ytmp, ALU.add)
                        self.dma(self.scr_yf[c], yf)
                    else:
                        self.dma(yf, self.scr_yf[c])
                        self.tt(ysum, PS[4][:, 0:512], ytmp, ALU.add)
                        self.tt(ysum, ysum, yf, ALU.add)
                        self.tt(ydx, xtok[:, c, :], Dg, ALU.mult, eng="pool")
                        self.tt(ysum, ysum, ydx, ALU.add)
                        self.tt(ysum, ysum, szb, ALU.mult)
                        self.act(ytmp, ysum, AF.Square, accum_out=ssq)
                        self.act(ssq, ssq, AF.Ln, bias=self.epsD[:, 0:1], scale=1.0 / 512.0)
                        self.act(ssq, ssq, AF.Exp, scale=-0.5)
                        self.stt(ynbs[sl], ysum, ssq, ngt, ALU.mult, ALU.mult)
            def stageT(c, sl):
                for j in range(4):
                    self.mm(PS[6][:, j * 128:(j + 1) * 128], ynbs[sl][:, j * 128:(j + 1) * 128], self.identb[:])
                self.cp(ygT[:, c % 4, :], PS[6][:, 0:512])
                if c % 4 == 0:
                    cbase = c
                    ncs = min(4, NCL - cbase)
                    for f in range(KD):
                        w = self.wload(self.wview(d["s_w_out"], f * 128, 128, k0=g * 4, K=4), 4, 128)
                        ps = PS[f % 2]
                        for j in range(4):
                            rhs = ygT[:, 0:ncs, j * 128:(j + 1) * 128]
                            self.mm(ps[:, 0:ncs * 128].rearrange("p (c t) -> p c t", t=128), w[:, j, :], rhs, start=(j == 0), stop=(j == 3))
                        self.xt_update(ps[:, 0:ncs * 128], f, cbase * 128, ncs * 128, g1(f))

            for dr in range(2):
                seq = self.chunk_order(dr)
                lat = [c for c in seq if c < NCL]
                self.S.call("pool", "memset", ap=H32, constant=0.0)
                self.S.call("pool", "memset", ap=Hb, constant=0.0)
                stageA(dr, lat[0], 0)
                stageS(dr, seq[0], 0)
                li = 0
                for ci, c in enumerate(seq):
                    if ci + 1 < len(seq) - 1 or (ci + 1 < len(seq) and False):
                        stageS(dr, seq[ci + 1], (ci + 1) % 2)
                    if c < NCL:
                        if li + 1 < len(lat):
                            stageA(dr, lat[li + 1], (li + 1) % 2)
                        stageB(dr, ci, c, li % 2, ci % 2)
                        if dr == 1 and li >= 1:
                            stageT(lat[li - 1], (ci - 1) % 2)
                        li += 1
                    else:
                        stageB(dr, ci, c, 0, ci % 2)
                if dr == 1:
                    stageT(lat[-1], (len(seq) - 1) % 2)

    def build(self):
        self.alloc()
        self.epsD = self.sb("epsD", [128, 1], F32)
        self.lnq = self.sb("lnq", [128, 1], F32)
        self.lnk = self.sb("lnk", [128, 1], F32)
        self.S.call("pool", "memset", ap=self.epsD[:], constant=EPS)
        self.S.call("pool", "memset", ap=self.lnq[:], constant=math.log(0.125))
        self.S.call("pool", "memset", ap=self.lnk[:], constant=math.log(128.0 ** -0.5))
        self.consts()
        self.mod_phase()
        stop = self.stop
        order = ["norm", "l0mix", "l0ffn", "l1mix", "full"]
        lvl = order.index(stop)
        for b in range(self.nb):
            for k in range(KD):
                self.dma(self.XT[:, k, :], self.d["xT"][b, k * 128:(k + 1) * 128, :])
            for l in range(2):
                if lvl < 1 + 2 * l:
                    break
                A1 = lambda k, isc, l=l: self.Amix[:, l, k, (2 if isc else b):(3 if isc else b + 1)]
                B1 = lambda k, isc, l=l: self.modT[:, l, k, (2 if isc else b):(3 if isc else b + 1)]
                self.norm(A1, B1)
                if l == 0:
                    self.even_mixer(l, b)
                else:
                    self.ssd_mixer(l, b)
                if lvl < 2 + 2 * l:
                    break
                A2 = lambda k, isc, l=l: self.Affn[:, l, k, (2 if isc else b):(3 if isc else b + 1)]
                B2 = lambda k, isc, l=l: self.modT[:, l, 24 + k, (2 if isc else b):(3 if isc else b + 1)]
                self.norm(A2, B2, ctx=(l == 0))
                self.ffn(l, b, do_ctx=(l == 0))
            ost = self.scrf[:, 1536:1536 + 2 * 512].rearrange("p (i n) -> p i n", n=512)
            cnt = [0]

            def outbuf(k, t0, n):
                return ost[:, k % 2, 0:n]
            tl = self.tiles(True, False)
            for ti, (t0, n, isc) in enumerate(tl):
                ps = self.ps(ti % 2)
                for k in range(KD):
                    sq = self.sqb[:, k % 2, 0:n]
                    self.act(sq, self.XT[:, k, t0:t0 + n], AF.Square)
                    self.mm(ps[:, 0:n], self.onesb[:], sq, start=(k == 0), stop=(k == KD - 1))
                rs = self.rsb[:, 0, 0:n]
                self.act(rs, ps[:, 0:n], AF.Ln, bias=self.epsD[:, 0:1], scale=1.0 / D)
                self.act(ps[:, 0:n], rs, AF.Exp, scale=-0.5)
                for k in range(KD):
                    tmp = ost[:, k % 2, 0:n]
                    self.stt(tmp, self.XT[:, k, t0:t0 + n], self.gfin[:, k:k + 1], ps[:, 0:n], ALU.mult, ALU.mult)
                    self.out_ops.append(self.dma(self.outT[b, k * 128:(k + 1) * 128, t0:t0 + n], tmp))
        self.S.lower(self.out_ops)
        self.st.close()
        return self.nc


def _pl(v, nch):
    return np.ascontiguousarray(np.asarray(v, np.float32).reshape(nch, 128).T)


def prep_shared(inp):
    f = lambda a: np.ascontiguousarray(np.asarray(a, np.float32))
    sh = {}
    sh["mod_w"] = f(inp["mod_w"])
    sh["modb"] = np.ascontiguousarray(np.stack([_pl(inp["mod_b"][l], 48) for l in range(2)], axis=1))
    sh["gmix"] = np.ascontiguousarray(np.stack([_pl(inp["norm_mix_g"][l], 8) for l in range(2)], axis=1))
    sh["gffn"] = np.ascontiguousarray(np.stack([_pl(inp["norm_ffn_g"][l], 8) for l in range(2)], axis=1))
    sh["gfin"] = _pl(inp["final_norm_g"], 8)
    sh["w_up"] = f(inp["ffn_w_up"])
    cw = np.asarray(inp["ffn_conv_w"], np.float32).reshape(2, 9, 44, 128)
    sh["cw"] = np.ascontiguousarray(cw.transpose(3, 0, 2, 1))
    sh["cb"] = np.ascontiguousarray(np.asarray(inp["ffn_conv_b"], np.float32).reshape(2, 44, 128).transpose(2, 0, 1))
    sh["w_down"] = f(inp["ffn_w_down"])
    sh["e_w_in"] = f(inp["even_w_in"][0])
    sh["a1"] = np.ascontiguousarray(np.concatenate([inp["gla_a1"][0, 0], inp["gla_a1"][0, 1]], axis=1).astype(np.float32))
    sh["a2"] = np.ascontiguousarray(np.asarray(inp["gla_a2"][0], np.float32).transpose(1, 0, 2))
    sh["ab"] = np.ascontiguousarray(np.asarray(inp["gla_ab"][0], np.float32).reshape(2, 2, 128).transpose(2, 0, 1))
    sh["mlcw"] = np.ascontiguousarray(np.asarray(inp["ml_conv_w"][0], np.float32).reshape(3, 8, 128).transpose(2, 1, 0))
    sh["mlcb"] = _pl(inp["ml_conv_b"][0], 8)
    sh["gateb"] = f(inp["ml_gate_b"][0]).reshape(1, 16)
    sh["glang"] = _pl(inp["gla_norm_g"][0], 4)
    sh["mlng"] = _pl(inp["ml_norm_g"][0], 4)
    sh["e_w_out"] = f(inp["even_w_out"][0])
    sh["s_w_in"] = f(inp["ssd_w_in"][0])
    sh["scw"] = np.ascontiguousarray(np.asarray(inp["ssd_conv_w"][0], np.float32).reshape(3, 24, 128).transpose(2, 1, 0))
    sh["scb"] = _pl(inp["ssd_conv_b"][0], 24)
    sh["sdtb"] = np.ascontiguousarray(np.asarray(inp["ssd_dt_bias"][0], np.float32).reshape(2, 4, 8).transpose(1, 0, 2).reshape(4, 16))
    sh["salog"] = np.ascontiguousarray(np.asarray(inp["ssd_a_log"][0], np.float32).reshape(2, 4, 8).transpose(1, 0, 2).reshape(4, 16))
    sh["sdrep"] = np.ascontiguousarray(np.repeat(np.asarray(inp["ssd_d"][0], np.float32), 64).reshape(1, 2048))
    sh["sng"] = f(inp["ssd_norm_g"][0]).reshape(1, 2048)
    sh["s_w_out"] = f(inp["ssd_w_out"][0])
    sh["c_identf"] = np.eye(128, dtype=np.float32)
    sh["c_triu"] = np.triu(np.ones((128, 128), np.float32))
    sh["c_tril"] = np.tril(np.ones((128, 128), np.float32))
    return sh


def core_inputs(inp, sh, b0, nb):
    x = np.asarray(inp["x"], np.float32)
    ctx = np.asarray(inp["ctx"], np.float32)
    m = dict(sh)
    xt = np.concatenate([x[b0:b0 + nb], ctx[b0:b0 + nb]], axis=1)
    m["xT"] = np.ascontiguousarray(xt.transpose(0, 2, 1))
    cs = np.concatenate([np.asarray(inp["c"], np.float32)[b0:b0 + nb], np.zeros((2 - nb, D), np.float32) if nb < 2 else np.zeros((0, D), np.float32),
                         np.asarray(inp["c_ctx"], np.float32)[None]], axis=0)
    m["cT"] = np.ascontiguousarray(cs.reshape(3, KD, 128).transpose(2, 1, 0))
    return m


_CACHE = {}


def run(inp, n_cores=8, nb=2, RB=16, stop="full", sim=False):
    TL = inp["x"].shape[1]
    TC = inp["ctx"].shape[1]
    key = (TL, TC, RB, stop, nb)
    nc = Builder(TL, TC, RB, stop, nb).build()
    sh = prep_shared(inp)
    in_maps = [core_inputs(inp, sh, i * nb, nb) for i in range(n_cores)]
    if sim:
        from simrun import sim_run
        res = sim_run(nc, in_maps)
    else:
        res = run_bass_kernel_spmd(nc, in_maps, core_ids=list(range(n_cores))).results
    out = np.concatenate([r["outT"] for r in res], axis=0)
    return np.ascontiguousarray(out.transpose(0, 2, 1))


def kernel(**inputs):
    return run(inputs, n_cores=8, nb=2, RB=16, stop="full").astype(np.float32)
```

```python
import contextlib
import math
import os
import numpy as np
KDBG = set(os.environ.get("KDBG", "").split(","))
KLVL = int(os.environ.get("KLVL", "99"))
import concourse.bass as bass
import concourse.mybir as mybir
from concourse.bass_utils import run_bass_kernel_spmd

F32 = mybir.dt.float32
BF16 = mybir.dt.bfloat16
AF = mybir.ActivationFunctionType
ALU = mybir.AluOpType
AX = mybir.AxisListType

ENGS = ("pe", "act", "dve", "pool", "sp")
EPOCH = 30000
NDSEM = 24


def _prod(xs):
    r = 1
    for x in xs:
        r *= int(x)
    return r


def region(ap):
    t = ap.tensor
    cls = type(t).__name__
    if cls.startswith("DRam"):
        if not t.name.startswith("scr_"):
            return None
        ext = 1
        for st_, c_ in ap.ap:
            ext += (int(c_) - 1) * abs(int(st_))
        return (t.name, 0, 1, int(ap.offset), int(ap.offset) + ext)
    F = _prod(list(t.shape)[1:])
    off = int(ap.offset)
    p0 = off // F
    f0 = off % F
    dims = [(int(s), int(c)) for s, c in ap.ap]
    pn = 1
    if dims and dims[0][0] > 0 and dims[0][0] % F == 0:
        pn = (dims[0][1] - 1) * (dims[0][0] // F) + 1
        dims = dims[1:]
    elif dims and dims[0][1] == 1:
        dims = dims[1:]
    ext = 1
    for s, c in dims:
        ext += (c - 1) * abs(s)
    if cls.startswith("PSum"):
        return (t.name, p0, p0 + pn, 0, F)
    return (t.name, p0, p0 + pn, f0, f0 + ext)


class Op:
    __slots__ = ("eng", "fn", "deps", "dma", "marked", "sem", "val", "idx")


class Sched:
    def __init__(self, nc, same_engine_sync=True):
        self.nc = nc
        self.ops = []
        self.recs = {}
        self.same = same_engine_sync
        self.ndma = 0
        self.dma_ops = []

    def add(self, eng, fn, outs, ins, dma=False):
        op = Op()
        op.eng = eng
        op.fn = fn
        op.dma = dma
        op.marked = False
        op.idx = len(self.ops)
        deps = set()
        rr = [region(a) for a in ins]
        ww = [region(a) for a in outs]
        for r in rr:
            if r is None:
                continue
            psum = r[0].startswith("ps")
            for rec in self.recs.get(r[0], ()):
                if (rec[6] or (psum and rec[7] != eng)) and rec[1] < r[2] and r[1] < rec[2] and rec[3] < r[4] and r[3] < rec[4]:
                    deps.add(rec[5])
        for w in ww:
            if w is None:
                continue
            for rec in self.recs.get(w[0], ()):
                if rec[1] < w[2] and w[1] < rec[2] and rec[3] < w[4] and w[3] < rec[4]:
                    deps.add(rec[5])
        if dma:
            if self.ndma >= NDSEM:
                deps.add(self.dma_ops[self.ndma - NDSEM])
            self.dma_ops.append(op.idx)
            self.ndma += 1
        fd = set()
        for d in deps:
            o = self.ops[d]
            if o.eng == eng and not o.dma:
                if eng == "pe" or eng == "sp" or not self.same:
                    continue
            fd.add(d)
        op.deps = fd
        for d in fd:
            self.ops[d].marked = True
        self.ops.append(op)
        for w in ww:
            if w is None:
                continue
            lst = self.recs.setdefault(w[0], [])
            lst[:] = [rec for rec in lst if not (w[1] <= rec[1] and rec[2] <= w[2] and w[3] <= rec[3] and rec[4] <= w[4])]
            lst.append((w[0], w[1], w[2], w[3], w[4], op.idx, True, eng))
        for r in rr:
            if r is None:
                continue
            lst = self.recs.setdefault(r[0], [])
            lst[:] = [rec for rec in lst if not ((not rec[6]) and rec[7] == eng and not dma and not self.ops[rec[5]].dma
                                                 and r[1] <= rec[1] and rec[2] <= r[2] and r[3] <= rec[3] and rec[4] <= r[4])]
            lst.append((r[0], r[1], r[2], r[3], r[4], op.idx, False, eng))
        return op

    def call(self, eng, meth, **kw):
        outs, ins = [], []
        for k, v in kw.items():
            if isinstance(v, bass.AP):
                (outs if k in ("out", "accum_out", "ap") else ins).append(v)
        return self.add(eng, lambda e, m=meth, k=kw: getattr(e, m)(**k), outs, ins)

    def dma(self, eng, out, in_, **kw):
        return self.add(eng, lambda e, o=out, i=in_, k=kw: e.dma_start(out=o, in_=i, **k), [out], [in_], dma=True)

    def mm(self, out, lhsT, rhs, start=True, stop=True):
        return self.add("pe", lambda e: e.matmul(out, lhsT, rhs, start=start, stop=stop), [out], [lhsT, rhs])

    def lower(self, final_wait_ops=()):
        nc = self.nc
        fin = Op()
        fin.eng = "sp"
        fin.fn = None
        fin.dma = False
        fin.marked = False
        fin.idx = len(self.ops)
        fin.deps = set(o.idx for o in final_wait_ops)
        for d in fin.deps:
            self.ops[d].marked = True
        self.ops.append(fin)
        cnt = {e: 0 for e in ENGS}
        for op in self.ops:
            if not op.dma and op.marked:
                cnt[op.eng] += 1
        need = {e: max(1, (cnt[e] + EPOCH - 1) // EPOCH) for e in ENGS}
        with contextlib.ExitStack() as st:
            esems = {e: [st.enter_context(nc.semaphore("s_%s_%d" % (e, i))) for i in range(need[e])] for e in ENGS}
            dsems = [st.enter_context(nc.semaphore("s_dma_%d" % i)) for i in range(NDSEM)]
            c = {e: 0 for e in ENGS}
            dj = 0
            for op in self.ops:
                if op.dma:
                    op.sem = dsems[dj % NDSEM]
                    op.val = 16 * (dj // NDSEM + 1)
                    dj += 1
                elif op.marked:
                    k = c[op.eng]
                    c[op.eng] += 1
                    op.sem = esems[op.eng][k // EPOCH]
                    op.val = (k % EPOCH) + 1
            block = st.enter_context(nc.Block())
            per = {e: [op for op in self.ops if op.eng == e] for e in ENGS}
            ops = self.ops

            def emit(e, lst):
                waited = {}
                for op in lst:
                    ws = {}
                    for d in op.deps:
                        o = ops[d]
                        key = id(o.sem)
                        if key not in ws or ws[key][1] < o.val:
                            ws[key] = (o.sem, o.val)
                    for key, (sem, val) in ws.items():
                        if waited.get(key, 0) >= val:
                            continue
                        waited[key] = val
                        e.wait_ge(sem, val)
                    if op.fn is None:
                        continue
                    ins = op.fn(e)
                    if op.dma:
                        ins.then_inc(op.sem, 16)
                    elif op.marked:
                        ins.then_inc(op.sem, 1)

            @block.tensor
            def _(e):
                emit(e, per["pe"])

            @block.scalar
            def _(e):
                emit(e, per["act"])

            @block.vector
            def _(e):
                emit(e, per["dve"])

            @block.gpsimd
            def _(e):
                emit(e, per["pool"])

            @block.sync
            def _(e):
                emit(e, per["sp"])


D = 1024
KD = 8
GRID_W = 64
EPS = 1e-6
DFF = 2816
NFC = 22
EVEN_IN = 3600
ODD_IN = 5184
OFF_GQ, OFF_GK, OFF_GV, OFF_GG, OFF_MQ, OFF_MK, OFF_MV, OFF_MO, OFF_MG = 0, 256, 512, 1024, 1536, 2048, 2560, 3072, 3584
OFF_Z, OFF_X, OFF_B, OFF_C, OFF_DT = 0, 2048, 4096, 4608, 5120
WB = 1024
NEG = -1.0e30

DRAM_IN = [
    ("xT", [2, D, None]), ("cT", [128, KD, 3]),
    ("mod_w", [2, D, 6 * D]), ("modb", [128, 2, 48]), ("gmix", [128, 2, KD]), ("gffn", [128, 2, KD]), ("gfin", [128, KD]),
    ("w_up", [2, D, 2 * DFF]), ("cw", [128, 2, 44, 9]), ("cb", [128, 2, 44]), ("w_down", [2, DFF, D]),
    ("e_w_in", [D, EVEN_IN]), ("a1", [D, 32]), ("a2", [16, 2, 256]), ("ab", [128, 2, 2]),
    ("mlcw", [128, 8, 3]), ("mlcb", [128, 8]), ("gateb", [1, 16]), ("glang", [128, 4]), ("mlng", [128, 4]),
    ("e_w_out", [D, D]),
    ("s_w_in", [D, ODD_IN]), ("scw", [128, 24, 3]), ("scb", [128, 24]), ("sdtb", [4, 16]), ("salog", [4, 16]),
    ("sdrep", [1, 2048]), ("sng", [1, 2048]), ("s_w_out", [2048, D]),
    ("c_identf", [128, 128]), ("c_triu", [128, 128]), ("c_tril", [128, 128]),
]


class Builder:
    def __init__(self, TL, TC, RB, stop="full", nb=2):
        self.TL, self.TC, self.T = TL, TC, TL + TC
        self.NCL, self.NCC = TL // 128, TC // 128
        self.NCH = self.NCL + self.NCC
        self.ROWS = TL // GRID_W
        self.RB = min(RB, self.ROWS)
        self.stop = stop
        self.nb = nb
        self.nc = bass.Bass("TRN2", target_bir_lowering=False)
        self.S = Sched(self.nc)
        self.st = contextlib.ExitStack()
        self.d = {}
        for name, shp in DRAM_IN:
            shp = [self.T if s is None else s for s in shp]
            if name == "xT":
                shp[0] = nb
            self.d[name] = self.nc.dram_tensor(name, shp, F32, kind="ExternalInput").ap()
        self.outT = self.nc.dram_tensor("outT", [nb, D, TL], F32, kind="ExternalOutput").ap()
        self.psn = 0
        self.wi = 0
        self.out_ops = []

    def sb(self, name, shape, dt):
        return self.st.enter_context(self.nc.sbuf_tensor(name, shape, dt))

    def alloc(self):
        T = self.T
        self.XT = self.sb("XT", [128, KD, T], F32)
        self.hT = self.sb("hT", [128, KD, T], BF16)
        self.wst = self.sb("wst", [128, 2, WB], F32)
        self.wbf = self.sb("wbf", [128, 3, WB], BF16)
        self.PS = [self.st.enter_context(self.nc.psum_tensor("ps%d" % i, [128, 512], F32)) for i in range(8)]
        self.identf = self.sb("identf", [128, 128], F32)
        self.identb = self.sb("identb", [128, 128], BF16)
        self.triu = self.sb("triu", [128, 128], F32)
        self.tril = self.sb("tril", [128, 128], F32)
        self.onesf = self.sb("onesf", [128, 128], F32)
        self.onesb = self.sb("onesb", [128, 128], BF16)
        self.mskf = self.sb("mskf", [128, 128], BF16)
        self.mskb = self.sb("mskb", [128, 128], BF16)
        self.nmf = self.sb("nmf", [128, 128], BF16)
        self.nmb = self.sb("nmb", [128, 128], BF16)
        if "nomix" in KDBG:
            self.nmf32 = self.sb("nmf32", [128, 128], F32)
            self.nmb32 = self.sb("nmb32", [128, 128], F32)
        self.cTt = self.sb("cTt", [128, KD, 3], F32)
        self.scT = self.sb("scT", [128, KD, 3], F32)
        self.modT = self.sb("modT", [128, 2, 48, 3], F32)
        self.modb = self.sb("modbt", [128, 2, 48], F32)
        self.gmix = self.sb("gmixt", [128, 2, KD], F32)
        self.gffn = self.sb("gffnt", [128, 2, KD], F32)
        self.gfin = self.sb("gfint", [128, KD], F32)
        self.zero8 = self.sb("zero8", [128, KD], F32)
        self.Amix = self.sb("Amix", [128, 2, KD, 3], F32)
        self.Affn = self.sb("Affn", [128, 2, KD, 3], F32)
        self.cw = self.sb("cwt", [128, 2, 44, 9], BF16)
        self.cb = self.sb("cbt", [128, 2, 44], F32)
        self.a1b = self.sb("a1b", [128, KD, 32], BF16)
        self.nab = self.sb("nab", [128, 2, 2], F32)
        self.mlcw = self.sb("mlcwt", [128, 8, 3], F32)
        self.mlcb = self.sb("mlcbt", [128, 8], F32)
        self.gateb = self.sb("gatebt", [128, 16], F32)
        self.glang = self.sb("glangt", [128, 4], F32)
        self.mlng = self.sb("mlngt", [128, 4], F32)
        self.scw = self.sb("scwt", [128, 24, 3], F32)
        self.scb = self.sb("scbt", [128, 24], F32)
        self.SCRB = 25600 + 1152 + 640 + 512 + 1152
        self.SCRF = 4650
        self.scrb = self.sb("scrb", [128, self.SCRB], BF16)
        self.scrf = self.sb("scrf", [128, self.SCRF], F32)
        self.rsb = self.scrf[:, 0:1536].rearrange("p (i n) -> p i n", n=512)
        self.sqb = self.scrb[:, 0:1024].rearrange("p (i n) -> p i n", n=512)
        self.scr_yf = self.nc.dram_tensor("scr_yf", [self.NCL, 128, 512], BF16, kind="Internal").ap()

    def ps(self, i=None):
        if i is None:
            i = self.psn % 8
            self.psn += 1
        return self.PS[i]

    def act(self, out, in_, func, bias=0.0, scale=1.0, **kw):
        return self.S.call("act", "activation", out=out, in_=in_, func=func, bias=bias, scale=scale, **kw)

    def ts(self, out, in0, s1, s2=None, op0=ALU.mult, op1=None, eng="dve"):
        if op1 is None:
            if eng == "pool" and op0 == ALU.mult:
                return self.S.call(eng, "tensor_scalar", out=out, in0=in0, scalar1=s1, scalar2=0.0, op0=op0, op1=ALU.add)
            return self.S.call(eng, "tensor_scalar", out=out, in0=in0, scalar1=s1, scalar2=None, op0=op0)
        return self.S.call(eng, "tensor_scalar", out=out, in0=in0, scalar1=s1, scalar2=s2, op0=op0, op1=op1)

    def tt(self, out, in0, in1, op, eng="dve"):
        return self.S.call(eng, "tensor_tensor", out=out, in0=in0, in1=in1, op=op)

    def stt(self, out, in0, scalar, in1, op0, op1):
        return self.S.call("dve", "scalar_tensor_tensor", out=out, in0=in0, scalar=scalar, in1=in1, op0=op0, op1=op1)

    def cp(self, out, in_, eng="act"):
        if eng == "act":
            return self.act(out, in_, AF.Copy)
        return self.S.call(eng, "tensor_copy", out=out, in_=in_)

    def mm(self, *a, **k):
        return self.S.mm(*a, **k)

    def dma(self, out, in_, eng="sp"):
        return self.S.dma(eng, out, in_)

    def wload(self, src, K, C, eng="pool"):
        assert K * C <= WB
        i = self.wi
        self.wi += 1
        st = self.wst[:, i % 2, 0:K * C].rearrange("p (k c) -> p k c", c=C)
        bf = self.wbf[:, i % 3, 0:K * C].rearrange("p (k c) -> p k c", c=C)
        self.dma(st, src)
        self.S.call(eng, "tensor_copy", out=bf, in_=st)
        return bf

    def wview(self, w2d, c0, C, k0=0, K=KD):
        return w2d.rearrange("(k p) n -> p k n", p=128)[:, k0:k0 + K, c0:c0 + C]

    def tiles(self, lat=True, ctx=True):
        r = []
        if lat:
            for i in range(0, self.TL, 512):
                r.append((i, min(512, self.TL - i), False))
        if ctx:
            for i in range(0, self.TC, 512):
                r.append((self.TL + i, min(512, self.TC - i), True))
        return r

    def consts(self):
        d = self.d
        self.dma(self.identf[:], d["c_identf"])
        self.dma(self.triu[:], d["c_triu"])
        self.dma(self.tril[:], d["c_tril"])
        self.S.call("pool", "memset", ap=self.onesf[:], constant=1.0)
        self.S.call("pool", "memset", ap=self.onesb[:], constant=1.0)
        self.S.call("pool", "memset", ap=self.zero8[:], constant=0.0)
        self.cp(self.identb[:], self.identf[:], "dve")
        self.cp(self.mskf[:], self.triu[:], "dve")
        self.cp(self.mskb[:], self.tril[:], "dve")
        self.ts(self.nmf[:], self.triu[:], -1.0, -NEG, op0=ALU.add, op1=ALU.mult)
        self.ts(self.nmb[:], self.tril[:], -1.0, -NEG, op0=ALU.add, op1=ALU.mult)
        if "nomix" in KDBG:
            self.ts(self.nmf32[:], self.triu[:], -1.0, -NEG, op0=ALU.add, op1=ALU.mult)
            self.ts(self.nmb32[:], self.tril[:], -1.0, -NEG, op0=ALU.add, op1=ALU.mult)
        for nm, t in (("cT", self.cTt), ("modb", self.modb), ("gmix", self.gmix), ("gffn", self.gffn), ("gfin", self.gfin),
                      ("cb", self.cb), ("mlcw", self.mlcw), ("mlcb", self.mlcb),
                      ("glang", self.glang), ("mlng", self.mlng), ("scw", self.scw), ("scb", self.scb),
                      ):
            self.dma(t[:], d[nm])
        for l_ in range(2):
            cwst = self.wst[:, l_, 0:396].rearrange("p (c t) -> p c t", t=9)
            self.dma(cwst, d["cw"][:, l_])
            self.cp(self.cw[:, l_], cwst, "dve")
        self.dma(self.nab[:], d["ab"])
        self.ts(self.nab[:], self.nab[:], -1.0)

        self.dma(self.gateb[:], d["gateb"][0:1, :].to_broadcast([128, 16]))
        st = self.wst[:, 0, 0:256].rearrange("p (k c) -> p k c", c=32)
        self.dma(st, self.wview(d["a1"], 0, 32))
        self.cp(self.a1b[:], st, "pool")

    def mod_phase(self):
        d = self.d
        scb = self.scrb[:, 1024:1024 + KD * 3].rearrange("p (k s) -> p k s", s=3)
        self.act(scb, self.cTt[:], AF.Silu)
        for l in range(2):
            for jb in range(48):
                st = self.wload(self.wview(d["mod_w"][l], jb * 128, 128), KD, 128, eng=("pool" if jb % 2 else "dve"))
                ps = self.ps(jb % 2)
                for k in range(KD):
                    self.mm(ps[:, 0:3], st[:, k, :], scb[:, k, :], start=(k == 0), stop=(k == KD - 1))
                self.act(self.modT[:, l, jb, :], ps[:, 0:3], AF.Identity, bias=self.modb[:, l, jb:jb + 1])
            for (A, g, off) in ((self.Amix, self.gmix, 8), (self.Affn, self.gffn, 32)):
                self.ts(A[:, l], self.modT[:, l, off:off + 8, :], 1.0, op0=ALU.add)
                for s in range(3):
                    self.tt(A[:, l, :, s], A[:, l, :, s], g[:, l, :], ALU.mult)

    def norm(self, Afn, Bfn, lat=True, ctx=True, out_fp32=None):
        for ti, (t0, n, isc) in enumerate(self.tiles(lat, ctx)):
            ps = self.ps(ti % 2)
            for k in range(KD):
                sq = self.sqb[:, k % 2, 0:n]
                self.act(sq, self.XT[:, k, t0:t0 + n], AF.Square)
                self.mm(ps[:, 0:n], self.onesb[:], sq, start=(k == 0), stop=(k == KD - 1))
            rs = self.rsb[:, 0, 0:n]
            self.act(rs, ps[:, 0:n], AF.Ln, bias=self.epsD[:, 0:1], scale=1.0 / D)
            self.act(ps[:, 0:n], rs, AF.Exp, scale=-0.5)
            for k in range(KD):
                tmp = self.rsb[:, 1 + k % 2, 0:n] if out_fp32 is None else out_fp32(k, t0, n)
                self.tt(tmp, self.XT[:, k, t0:t0 + n], ps[:, 0:n], ALU.mult)
                if out_fp32 is None:
                    self.act(self.hT[:, k, t0:t0 + n], tmp, AF.Identity, bias=Bfn(k, isc), scale=Afn(k, isc))
                else:
                    self.ts(tmp, tmp, Afn(k, isc))

    def xt_update(self, ps_ap, f, t0, n, gate_ap):
        self.stt(self.XT[:, f, t0:t0 + n], ps_ap, gate_ap, self.XT[:, f, t0:t0 + n], ALU.mult, ALU.add)

    def ffn(self, l, s_lat, do_ctx):
        d = self.d
        TL, TC, RB = self.TL, self.TC, self.RB
        nblk = self.ROWS // RB
        NPAD = (RB + 2) * 66
        o = 0
        upad = []
        for i in range(2):
            upad.append(self.scrb[:, o:o + NPAD].rearrange("p (r w) -> p r w", w=66))
            o += NPAD
        upc = []
        for i in range(2):
            upc.append(self.scrb[:, o:o + TC + 2])
            o += TC + 2
        NA = RB * 64 + TC
        actb = self.scrb[:, o:o + 11 * NA].rearrange("p (c t) -> p c t", t=NA)
        o += 11 * NA
        sg = self.scrb[:, o:o + NA]
        o += NA
        dgs = []
        for i in range(2):
            dgs.append(self.scrb[:, o:o + 9 * 128].rearrange("p (a m) -> p a m", m=128))
            o += 9 * 128
        ident9 = self.scrb[:, o:o + 9 * 128].rearrange("p (a m) -> p a m", m=128)
        o += 9 * 128
        assert o <= self.SCRB, o
        for tap in range(9):
            self.cp(ident9[:, tap, :], self.identb[:], "pool")
        for i in range(2):
            self.S.call("pool", "memset", ap=upad[i], constant=0.0)
            self.S.call("pool", "memset", ap=upc[i], constant=0.0)
        g2 = lambda f, isc: self.modT[:, l, 40 + f, (2 if isc else s_lat):(3 if isc else s_lat + 1)]
        it = 0
        for blk in range(nblk):
            r0 = blk * RB
            ra, rb = max(r0 - 1, 0), min(r0 + RB + 1, self.ROWS)
            nr = rb - ra
            ntok = nr * 64
            urow0 = ra - (r0 - 1)
            with_ctx = do_ctx and blk == 0
            for half in range(2):
                items = [(cc, typ) for cc in range(11) for typ in ("g", "a")]

                def stageU(idx, it_):
                    cc, typ = items[idx]
                    cidx = half * 11 + cc + (NFC if typ == "g" else 0)
                    w = self.wload(self.wview(d["w_up"][l], cidx * 128, 128), KD, 128, eng="dve")
                    up = upad[it_ % 2]
                    uc = upc[it_ % 2]
                    dg = dgs[it_ % 2]
                    bset = (it_ % 2) * 3
                    self.tt(dg, ident9, self.cw[:, l, cidx, :].unsqueeze(2).to_broadcast([128, 9, 128]), ALU.mult, eng="pool")
                    if blk == 0:
                        self.S.call("pool", "memset", ap=up[:, 0:1, :], constant=0.0)
                    if blk == nblk - 1:
                        self.S.call("pool", "memset", ap=up[:, RB + 1:RB + 2, :], constant=0.0)
                    pieces = [(q, min(512, ntok - q)) for q in range(0, ntok, 512)]
                    for pi, (q, n) in enumerate(pieces):
                        ps = self.PS[bset + pi]
                        for k in range(KD):
                            self.mm(ps[:, 0:n], w[:, k, :], self.hT[:, k, ra * 64 + q:ra * 64 + q + n], start=(k == 0), stop=(k == KD - 1))
                        self.act(up[:, urow0 + q // 64:urow0 + (q + n) // 64, 1:65],
                                 ps[:, 0:n].rearrange("p (r w) -> p r w", w=64), AF.Copy)
                    if with_ctx:
                        psc = self.PS[bset + 2]
                        assert TC <= 256 and ntok - 1024 <= 256
                        for k in range(KD):
                            self.mm(psc[:, 256:256 + TC], w[:, k, :], self.hT[:, k, TL:TL + TC], start=(k == 0), stop=(k == KD - 1))
                        self.act(uc[:, 1:1 + TC], psc[:, 256:256 + TC], AF.Copy)

                def stageC(idx, it_):
                    cc, typ = items[idx]
                    cidx = half * 11 + cc + (NFC if typ == "g" else 0)
                    up = upad[it_ % 2]
                    uc = upc[it_ % 2]
                    dg = dgs[it_ % 2]
                    for q in range(0, RB * 64, 512):
                        n = min(512, RB * 64 - q)
                        nrow = n // 64
                        i0 = q // 64
                        pc = self.PS[6 + (self.psn % 2)]
                        self.psn += 1
                        for tap in range(9):
                            dr, dw = tap // 3, tap % 3
                            self.mm(pc[:, 0:n].rearrange("p (r w) -> p r w", w=64), dg[:, tap, :],
                                    up[:, i0 + dr:i0 + dr + nrow, dw:dw + 64], start=(tap == 0), stop=(tap == 8))
                        self._ffn_evac(typ, pc[:, 0:n], q, n, cc, cidx, l, actb, sg)
                    if with_ctx:
                        pc = self.PS[6 + (self.psn % 2)]
                        self.psn += 1
                        for dw in range(3):
                            self.mm(pc[:, 0:TC], dg[:, 3 + dw, :], uc[:, dw:dw + TC], start=(dw == 0), stop=(dw == 2))
                        self._ffn_evac(typ, pc[:, 0:TC], RB * 64, TC, cc, cidx, l, actb, sg)

                stageU(0, it)
                for idx in range(len(items)):
                    if idx + 1 < len(items):
                        stageU(idx + 1, it + idx + 1)
                    stageC(idx, it + idx)
                it += len(items)
                for f in range(KD):
                    w1 = self.wload(self.wview(d["w_down"][l], f * 128, 128, k0=half * 11, K=8), 8, 128, eng="dve")
                    w2 = self.wload(self.wview(d["w_down"][l], f * 128, 128, k0=half * 11 + 8, K=3), 3, 128, eng="dve")
                    segs = [(q, min(512, RB * 64 - q), False) for q in range(0, RB * 64, 512)]
                    if with_ctx:
                        segs.append((RB * 64, TC, True))
                    for (q, n, isc) in segs:
                        ps = self.PS[self.psn % 6]
                        self.psn += 1
                        for c2 in range(11):
                            wk = w1[:, c2, :] if c2 < 8 else w2[:, c2 - 8, :]
                            self.mm(ps[:, 0:n], wk, actb[:, c2, q:q + n], start=(c2 == 0), stop=(c2 == 10))
                        t0 = (TL + 0) if isc else (r0 * 64 + q)
                        self.xt_update(ps[:, 0:n], f, t0, n, g2(f, isc))

    def _ffn_evac(self, typ, pc, q, n, cc, cidx, l, actb, sg):
        if typ == "g":
            self.act(sg[:, q:q + n], pc, AF.Silu, bias=self.cb[:, l, cidx:cidx + 1])
        else:
            self.stt(actb[:, cc, q:q + n], pc, self.cb[:, l, cidx:cidx + 1], sg[:, q:q + n], ALU.add, ALU.mult)

    def chunk_order(self, d):
        lat = list(range(self.NCL))
        ctx = list(range(self.NCL, self.NCH))
        return (ctx + lat) if d == 0 else (ctx[::-1] + lat[::-1])

    def conv1d(self, w_cols, cpad, acc, wv, bv, dests, func, post_scale=None):
        TL, TC = self.TL, self.TC
        for ti, (t0, n, isc) in enumerate(self.tiles()):
            ps = self.PS[6 + ti % 2]
            for k in range(KD):
                self.mm(ps[:, 0:n], w_cols[:, k, :], self.hT[:, k, t0:t0 + n], start=(k == 0), stop=(k == KD - 1))
            c0 = (t0 + 1) if not isc else (t0 + 3)
            self.act(cpad[:, c0:c0 + n], ps[:, 0:n], AF.Copy)
        for ti, (t0, n, isc) in enumerate(self.tiles()):
            c0 = (t0 + 1) if not isc else (t0 + 3)
            a = acc[:, 0, 0:n]
            self.ts(a, cpad[:, c0 - 1:c0 - 1 + n], wv[:, 0:1], bv, op0=ALU.mult, op1=ALU.add)
            self.stt(a, cpad[:, c0:c0 + n], wv[:, 1:2], a, ALU.mult, ALU.add)
            self.stt(a, cpad[:, c0 + 1:c0 + 1 + n], wv[:, 2:3], a, ALU.mult, ALU.add)
            self.act(dests(t0, n), a, func)

    def zero_cpad(self, cpad):
        TL, TC = self.TL, self.TC
        self.S.call("pool", "memset", ap=cpad[:, 0:1], constant=0.0)
        self.S.call("pool", "memset", ap=cpad[:, TL + 1:TL + 3], constant=0.0)
        self.S.call("pool", "memset", ap=cpad[:, TL + TC + 3:TL + TC + 4], constant=0.0)

    def even_mixer(self, l, s_lat):
        d = self.d
        T, TL, TC, NCH = self.T, self.TL, self.TC, self.NCH
        W = d["e_w_in"]
        g1 = lambda f, isc: self.modT[:, l, 16 + f, (2 if isc else s_lat):(3 if isc else s_lat + 1)]
        samp = lambda c: c >= self.NCL
        ob = [0]

        def cb_(n):
            a = self.scrb[:, ob[0]:ob[0] + n]
            ob[0] += n
            return a
        of_ = [0]

        def cf_(n):
            a = self.scrf[:, of_[0]:of_[0] + n]
            of_[0] += n
            return a
        qT = cb_(T)
        kT = cb_(T)
        sgT = cb_(2 * T).rearrange("p (j t) -> p j t", t=T)
        vtok = cb_(NCH * 256).rearrange("p (c v) -> p c v", v=256)
        ofw = cb_(NCH * 256).rearrange("p (c v) -> p c v", v=256)
        OT = cb_(2 * T).rearrange("p (j t) -> p j t", t=T)
        rsbb = cb_(128)
        a2l = cb_(256).rearrange("p (d c) -> p d c", c=128)
        qts = [cb_(128) for _ in range(2)]
        kt = cb_(128)
        ktoks = [cb_(128) for _ in range(2)]
        ktok = ktoks[0]
        ATms = [cb_(256).rearrange("p (j t) -> p j t", t=128) for _ in range(2)]
        onbs = [cb_(256).rearrange("p (j t) -> p j t", t=128) for _ in range(2)]
        Sb = cb_(128)
        vaugs = [cb_(130) for _ in range(2)]
        Cb = cb_(130)
        STms = [cb_(128) for _ in range(2)]
        hns = [cb_(128) for _ in range(2)]
        assert ob[0] <= self.SCRB, ob[0]
        gla0 = of_[0]
        e1 = cf_(128)
        sp = cf_(128)
        Ss = cf_(128)
        u = cf_(128)
        EQ = cf_(128)
        EK = cf_(128)
        S32 = cf_(128)
        tmpS = cf_(128)
        osum = cf_(256).rearrange("p (j t) -> p j t", t=128)
        sq2 = cf_(256).rearrange("p (j t) -> p j t", t=128)
        glaend = of_[0]
        of_[0] = gla0
        cpad = cf_(T + 4)
        of_[0] = max(of_[0], glaend)
        ssq = cf_(2)
        decs = [cf_(1) for _ in range(2)]
        C32 = cf_(130)
        gt = cf_(NCH * 16).rearrange("p (c g) -> p c g", g=16)
        spg = cf_(NCH * 8).rearrange("p (c g) -> p c g", g=8)
        cums = cf_(NCH * 16).rearrange("p (c g) -> p c g", g=16)
        wts = cf_(NCH * 8).rearrange("p (c g) -> p c g", g=8)
        ebs = cf_(NCH * 8).rearrange("p (c g) -> p c g", g=8)
        ets = cf_(NCH * 8).rearrange("p (c g) -> p c g", g=8)
        den = cf_(1)
        fac = cf_(1)
        hb = cf_(128)
        hss = [cf_(128) for _ in range(2)]
        tmpC = cf_(130)
        ssq1 = cf_(1)
        acc = cf_(512).rearrange("p (i n) -> p i n", n=512)
        assert of_[0] <= self.SCRF, of_[0]
        PS = self.PS

        for hp in range(2):
            for (dst, c0, fn) in ((qT, OFF_GQ + hp * 128, AF.Copy), (kT, OFF_GK + hp * 128, AF.Copy),
                                  (sgT[:, 0, :], OFF_GG + hp * 256, AF.Silu), (sgT[:, 1, :], OFF_GG + hp * 256 + 128, AF.Silu)):
                w = self.wload(self.wview(W, c0, 128), KD, 128)
                for ti, (t0, n, isc) in enumerate(self.tiles()):
                    ps = PS[6 + ti % 2]
                    for k in range(KD):
                        self.mm(ps[:, 0:n], w[:, k, :], self.hT[:, k, t0:t0 + n], start=(k == 0), stop=(k == KD - 1))
                    self.act(dst[:, t0:t0 + n], ps[:, 0:n], fn)
            for j in range(2):
                w = self.wload(self.wview(W, OFF_GV + hp * 256 + j * 128, 128), KD, 128)
                for c in range(NCH):
                    ps = PS[6 + c % 2]
                    for k in range(KD):
                        self.mm(ps[:, 0:128], self.hT[:, k, c * 128:(c + 1) * 128], w[:, k, :], start=(k == 0), stop=(k == KD - 1))
                    self.cp(vtok[:, c, j * 128:(j + 1) * 128], ps[:, 0:128])
            PSA = (PS[3], PS[7])
            PSO = (PS[4], PS[6])
            a2t = []
            for dr_ in range(2):
                i_ = self.wi
                self.wi += 1
                st_ = self.wst[0:16, i_ % 2, 0:128]
                bf_ = self.wbf[0:16, i_ % 3, 0:128]
                self.dma(st_, d["a2"][:, dr_, hp * 128:(hp + 1) * 128])
                self.cp(a2l[0:16, dr_, :], st_, "dve")

            def glaA(dr, c, sl):
                msk = self.mskf if dr == 0 else self.mskb
                tk = slice(c * 128, (c + 1) * 128)
                for k in range(KD):
                    self.mm(PS[0][0:16, 0:128], self.a1b[:, k, dr * 16:(dr + 1) * 16], self.hT[:, k, tk], start=(k == 0), stop=(k == KD - 1))
                self.cp(rsbb[0:16, :], PS[0][0:16, 0:128])
                self.mm(PS[0][:, 128:256], a2l[0:16, dr, :], rsbb[0:16, :])
                self.act(e1, PS[0][:, 128:256], AF.Exp, bias=self.nab[:, dr, hp:hp + 1], scale=-1.0)
                self.act(sp, e1, AF.Ln, bias=1.0)
                self.S.call("dve", "tensor_tensor_scan", out=Ss, data0=self.onesf[:], data1=sp, initial=0.0, op0=ALU.mult, op1=ALU.add)
                if dr == 0:
                    uu = Ss
                else:
                    self.ts(u, Ss, -1.0, Ss[:, 127:128], op0=ALU.mult, op1=ALU.add)
                    self.tt(u, u, sp, ALU.add, eng="pool")
                    uu = u
                self.act(EQ, uu, AF.Exp, bias=self.lnq[:, 0:1], scale=-1.0 / 16.0)
                self.act(EK, uu, AF.Exp, scale=1.0 / 16.0)
                self.act(decs[sl], Ss[:, 127:128], AF.Exp, scale=-1.0 / 16.0)
                self.tt(qts[sl], qT[:, tk], EQ, ALU.mult)
                self.tt(kt, kT[:, tk], EK, ALU.mult, eng="pool")
                self.mm(PS[2][:, 0:128], kt, self.identb[:])
                self.cp(ktoks[sl], PS[2][:, 0:128])
                for j in range(2):
                    hs_ = slice(j * 64, (j + 1) * 64)
                    self.mm(PSA[j][:, 0:128], kt[hs_, :], qts[sl][hs_, :])
                for j in range(2):
                    self.tt(ATms[sl][:, j, :], PSA[j][:, 0:128], msk[:], ALU.mult)

            def glaB(dr, ci, c, sl):
                tk = slice(c * 128, (c + 1) * 128)
                qt_, ktok_, ATm_, dec_ = qts[sl], ktoks[sl], ATms[sl], decs[sl]
                for j in range(2):
                    hs_ = slice(j * 64, (j + 1) * 64)
                    self.mm(PSO[j][:, 0:128], ATm_[:, j, :], vtok[:, c, j * 128:(j + 1) * 128], start=True, stop=False)
                    self.mm(PSO[j][:, 0:128], qt_[hs_, :], Sb[hs_, :], start=False, stop=True)
                if ci < NCH - 1:
                    for j in range(2):
                        self.mm(PS[1][j * 64:(j + 1) * 64, 0:128], ktok_[:, j * 64:(j + 1) * 64], vtok[:, c, j * 128:(j + 1) * 128])
                    self.tt(tmpS, S32, PS[1][:, 0:128], ALU.add)
                    self.act(Sb, tmpS, AF.Copy, scale=dec_)
                    self.ts(S32, tmpS, dec_)
                if dr == 0:
                    for j in range(2):
                        self.cp(ofw[:, c, j * 128:(j + 1) * 128], PSO[j][:, 0:128])
                else:
                    for j in range(2):
                        self.tt(osum[:, j, :], PSO[j][:, 0:128], ofw[:, c, j * 128:(j + 1) * 128], ALU.add)
                    self.tt(sq2, osum, osum, ALU.mult, eng="pool")
                    self.S.call("dve", "tensor_reduce", out=ssq, in_=sq2, axis=AX.X, op=ALU.add)
                    self.act(ssq, ssq, AF.Ln, bias=self.epsD[:, 0:1], scale=1.0 / 128.0)
                    self.act(ssq, ssq, AF.Exp, scale=-0.5)
                    self.tt(onbs[sl], osum, ssq.unsqueeze(2).to_broadcast([128, 2, 128]), ALU.mult)

            def glaT(c, sl):
                tk = slice(c * 128, (c + 1) * 128)
                for j in range(2):
                    self.mm(PS[5][:, j * 128:(j + 1) * 128], onbs[sl][:, j, :], self.identb[:])
                for j in range(2):
                    self.stt(OT[:, j, tk], PS[5][:, j * 128:(j + 1) * 128], self.glang[:, hp * 2 + j:hp * 2 + j + 1], sgT[:, j, tk], ALU.mult, ALU.mult)

            for dr in range(2):
                seq = self.chunk_order(dr)
                self.S.call("pool", "memset", ap=S32, constant=0.0)
                self.S.call("pool", "memset", ap=Sb, constant=0.0)
                glaA(dr, seq[0], 0)
                for ci, c in enumerate(seq):
                    if ci + 1 < len(seq):
                        glaA(dr, seq[ci + 1], (ci + 1) % 2)
                    glaB(dr, ci, c, ci % 2)
                    if dr == 1 and ci >= 1:
                        glaT(seq[ci - 1], (ci - 1) % 2)
                if dr == 1:
                    glaT(seq[-1], (len(seq) - 1) % 2)
            for f in range(KD):
                w = self.wload(self.wview(d["e_w_out"], f * 128, 128, k0=hp * 2, K=2), 2, 128)
                for ti, (t0, n, isc) in enumerate(self.tiles()):
                    ps = PS[6 + ti % 2]
                    for j in range(2):
                        self.mm(ps[:, 0:n], w[:, j, :], OT[:, j, t0:t0 + n], start=(j == 0), stop=(j == 1))
                    self.xt_update(ps[:, 0:n], f, t0, n, g1(f, isc))

        w = self.wload(self.wview(W, OFF_MG, 16), KD, 16)
        for c in range(NCH):
            ps = PS[c % 2]
            for k in range(KD):
                self.mm(ps[:, 0:16], self.hT[:, k, c * 128:(c + 1) * 128], w[:, k, :], start=(k == 0), stop=(k == KD - 1))
            self.tt(gt[:, c, :], ps[:, 0:16], self.gateb[:], ALU.add)
        for dr in range(2):
            self.act(spg[:, :, dr * 4:(dr + 1) * 4], gt[:, :, dr * 8 + 4:dr * 8 + 8], AF.Exp, scale=-1.0)
        self.act(spg, spg, AF.Ln, bias=1.0)
        assert NCH * 16 <= 512
        for c in range(NCH):
            for dr in range(2):
                tri = self.triu if dr == 0 else self.tril
                self.mm(PS[2][:, c * 16 + dr * 8:c * 16 + dr * 8 + 4], tri[:], spg[:, c, dr * 4:(dr + 1) * 4])
                self.mm(PS[2][:, c * 16 + dr * 8 + 4:c * 16 + dr * 8 + 8], self.onesf[:], spg[:, c, dr * 4:(dr + 1) * 4])
        self.ts(cums.rearrange("p c g -> p (c g)"), PS[2][:, 0:NCH * 16], -1.0)
        cv = cums.rearrange("p c (d g) -> p c d g", d=2)
        gv = gt.rearrange("p c (d g) -> p c d g", d=2)
        wv_ = wts.rearrange("p c (d g) -> p c d g", d=2)
        ev_ = ebs.rearrange("p c (d g) -> p c d g", d=2)
        tv_ = ets.rearrange("p c (d g) -> p c d g", d=2)
        for dr in range(2):
            self.tt(wv_[:, :, dr, :], gv[:, :, dr, 0:4], cv[:, :, dr, 0:4], ALU.subtract)
            self.act(wv_[:, :, dr, :], wv_[:, :, dr, :], AF.Exp, bias=self.lnk[:, 0:1])
            self.act(ev_[:, :, dr, :], cv[:, :, dr, 0:4], AF.Exp)
            self.act(tv_[:, :, dr, :], cv[:, :, dr, 4:8], AF.Exp)
        self.zero_cpad(cpad)
        for m in range(4):
            w = self.wload(self.wview(W, OFF_MQ + m * 128, 128), KD, 128)
            self.conv1d(w, cpad, acc, self.mlcw[:, m, :], self.mlcb[:, m:m + 1], lambda t0, n: qT[:, t0:t0 + n], AF.Silu)
            w = self.wload(self.wview(W, OFF_MK + m * 128, 128), KD, 128)
            self.conv1d(w, cpad, acc, self.mlcw[:, 4 + m, :], self.mlcb[:, 4 + m:5 + m], lambda t0, n: kT[:, t0:t0 + n], AF.Silu)
            vt = vtok.rearrange("p c (j v) -> p c j v", j=2)[:, :, 0, :]
            smo = vtok.rearrange("p c (j v) -> p c j v", j=2)[:, :, 1, :]
            hf = ofw.rearrange("p c (j v) -> p c j v", j=2)[:, :, 0, :]
            for (dst, c0, fn) in ((vt, OFF_MV + m * 128, AF.Copy), (smo, OFF_MO + m * 128, AF.Sigmoid)):
                w = self.wload(self.wview(W, c0, 128), KD, 128)
                for c in range(NCH):
                    ps = PS[6 + c % 2]
                    for k in range(KD):
                        self.mm(ps[:, 0:128], self.hT[:, k, c * 128:(c + 1) * 128], w[:, k, :], start=(k == 0), stop=(k == KD - 1))
                    self.act(dst[:, c, :], ps[:, 0:128], fn)
            OTm = OT[:, m % 2, :]
            def mlA(dr, c, sl):
                msk = self.mskf if dr == 0 else self.mskb
                tk = slice(c * 128, (c + 1) * 128)
                wcol = wv_[:, c, dr, m:m + 1]
                self.mm(PS[3][:, 0:128], kT[:, tk], qT[:, tk])
                self.tt(STms[sl], PS[3][:, 0:128], msk[:], ALU.mult)
                self.ts(vaugs[sl][:, 0:128], vt[:, c, :], wcol, eng="pool")
                self.cp(vaugs[sl][:, 128:129], wcol, "pool")
                self.mm(PS[2][:, 0:128], kT[:, tk], self.identb[:])
                self.cp(ktoks[sl], PS[2][:, 0:128])

            def mlB(dr, ci, c, sl):
                tk = slice(c * 128, (c + 1) * 128)
                ecol = ev_[:, c, dr, m:m + 1]
                tcol = tv_[:, c, dr, m:m + 1]
                vaug_, STm_, ktok_ = vaugs[sl], STms[sl], ktoks[sl]
                PSN = PS[4] if sl == 0 else PS[6]
                hs = hss[sl]
                hn = hns[sl]
                self.mm(PSN[:, 0:129], STm_, vaug_[:, 0:129], start=True, stop=False)
                self.mm(PSN[:, 0:129], qT[:, tk], Cb[:, 0:129], start=False, stop=True)
                if ci < NCH - 1:
                    self.mm(PS[7][:, 0:129], ktok_, vaug_[:, 0:129])
                    self.tt(tmpC[:, 0:129], C32[:, 0:129], PS[7][:, 0:129], ALU.add)
                    self.act(Cb[:, 0:129], tmpC[:, 0:129], AF.Copy, scale=tcol)
                    self.ts(C32[:, 0:129], tmpC[:, 0:129], tcol)
                self.act(den, PSN[:, 128:129], AF.Abs, scale=ecol)
                self.ts(den, den, 1.0, op0=ALU.max)
                self.S.call("dve", "reciprocal", out=den, in_=den)
                self.tt(fac, den, ecol, ALU.mult)
                if dr == 0:
                    self.ts(hf[:, c, :], PSN[:, 0:128], fac)
                else:
                    self.stt(hs, PSN[:, 0:128], fac, hf[:, c, :], ALU.mult, ALU.add)
                    self.tt(hs, hs, smo[:, c, :], ALU.mult, eng="pool")
                    self.act(hb, hs, AF.Square, accum_out=ssq1)
                    self.act(ssq1, ssq1, AF.Ln, bias=self.epsD[:, 0:1], scale=1.0 / 128.0)
                    self.act(ssq1, ssq1, AF.Exp, scale=-0.5)
                    self.ts(hn, hs, ssq1)

            def mlT(c, sl):
                tk = slice(c * 128, (c + 1) * 128)
                self.mm(PS[5][:, 0:128], hns[sl], self.identb[:])
                self.ts(OTm[:, tk], PS[5][:, 0:128], self.mlng[:, m:m + 1])

            for dr in range(2):
                seq = self.chunk_order(dr)
                self.S.call("pool", "memset", ap=C32, constant=0.0)
                self.S.call("pool", "memset", ap=Cb, constant=0.0)
                mlA(dr, seq[0], 0)
                for ci, c in enumerate(seq):
                    if ci + 1 < len(seq):
                        mlA(dr, seq[ci + 1], (ci + 1) % 2)
                    mlB(dr, ci, c, ci % 2)
                    if dr == 1 and ci >= 1:
                        mlT(seq[ci - 1], (ci - 1) % 2)
                if dr == 1:
                    mlT(seq[-1], (len(seq) - 1) % 2)
            for f in (range(KD) if m % 2 == 1 else ()):
                w = self.wload(self.wview(d["e_w_out"], f * 128, 128, k0=4 + m - 1, K=2), 2, 128)
                for ti, (t0, n, isc) in enumerate(self.tiles()):
                    ps = PS[6 + ti % 2]
                    for j in range(2):
                        self.mm(ps[:, 0:n], w[:, j, :], OT[:, j, t0:t0 + n], start=(j == 0), stop=(j == 1))
                    self.xt_update(ps[:, 0:n], f, t0, n, g1(f, isc))

    def ssd_mixer(self, l, s_lat):
        d = self.d
        T, TL, TC, NCH, NCL = self.T, self.TL, self.TC, self.NCH, self.NCL
        W = d["s_w_in"]
        g1 = lambda f: self.modT[:, l, 16 + f, s_lat:s_lat + 1]
        PS = self.PS
        ob = [0]

        def cb_(n):
            a = self.scrb[:, ob[0]:ob[0] + n]
            ob[0] += n
            return a
        of_ = [0]

        def cf_(n):
            a = self.scrf[:, of_[0]:of_[0] + n]
            of_[0] += n
            return a
        xtok = cb_(NCH * 512).rearrange("p (c v) -> p c v", v=512)
        BT = cb_(T)
        CT = cb_(T)
        wz = cb_(KD * 512).rearrange("p (k c) -> p k c", c=512)
        Ebuf = cb_(1024).rearrange("p (h t) -> p h t", t=128)
        Wt = [cb_(1024).rearrange("p (h t) -> p h t", t=128) for _ in range(2)]
        CBTb = [cb_(128) for _ in range(2)]
        Btks = [cb_(128) for _ in range(2)]
        xhs = [cb_(512) for _ in range(2)]
        Hb = cb_(512)
        szb = cb_(512)
        ynbs = [cb_(512) for _ in range(2)]
        yfs = [cb_(512) for _ in range(2)]
        ahi = cb_(NCH * 16).rearrange("p (c g) -> p c g", g=16)
        alo = cb_(NCH * 16).rearrange("p (c g) -> p c g", g=16)
        ovl0 = ob[0]
        ygT = cb_(4 * 512).rearrange("p (c v) -> p c v", v=512)
        ob[0] = ovl0
        xTc = [cb_(T)] * 2
        ob[0] = max(ob[0], ovl0 + 2048)
        assert ob[0] <= self.SCRB, ob[0]
        NG = 16
        dta = cf_(NCH * NG).rearrange("p (c g) -> p c g", g=NG)
        bia = cf_(NCH * NG).rearrange("p (c g) -> p c g", g=NG)
        ecu = cf_(NCH * NG).rearrange("p (c g) -> p c g", g=NG)
        tmpA = cf_(NCH * NG).rearrange("p (c g) -> p c g", g=NG)
        tmpB = cf_(NCH * NG).rearrange("p (c g) -> p c g", g=NG)
        wen = tmpA
        eto = tmpB
        Ab = cf_(NG)
        dtb = cf_(NG)
        ssq = cf_(1)
        ov = of_[0]
        H32 = cf_(512)
        ytmp = cf_(512)
        ysum = cf_(512)
        ydx = cf_(512)
        Dg = cf_(512)
        ngt = cf_(512)
        e1 = of_[0]
        of_[0] = ov
        acc = cf_(512).rearrange("p (i n) -> p i n", n=512)
        cpad = cf_(T + 4)
        of_[0] = max(of_[0], e1)
        assert of_[0] <= self.SCRF, of_[0]
        fl = lambda a: a.rearrange("p c g -> p (c g)")
        for g in range(4):
            if KLVL <= 0:
                return
            self.zero_cpad(cpad)
            self.dma(Ab, d["salog"][g:g + 1, :].to_broadcast([128, NG]))
            self.dma(dtb, d["sdtb"][g:g + 1, :].to_broadcast([128, NG]))
            self.act(Ab, Ab, AF.Exp)
            self.ts(Ab, Ab, -1.0)
            if KLVL <= 1:
                continue
            wd = []
            for dr in range(2):
                wd.append(self.wload(self.wview(W, OFF_DT + dr * 32 + g * 8, 8), KD, 8))
            for c in range(NCH):
                ps = PS[c % 2]
                for dr in range(2):
                    for k in range(KD):
                        self.mm(ps[:, dr * 8:(dr + 1) * 8], self.hT[:, k, c * 128:(c + 1) * 128], wd[dr][:, k, :], start=(k == 0), stop=(k == KD - 1))
                self.tt(tmpA[:, c, :], ps[:, 0:NG], dtb, ALU.add)
            if KLVL <= 2:
                continue
            self.act(fl(tmpA), fl(tmpA), AF.Exp)
            self.act(fl(tmpA), fl(tmpA), AF.Ln, bias=1.0)
            self.act(fl(tmpB), fl(tmpA), AF.Ln)
            self.tt(dta.rearrange("p c g -> p g c"), tmpA.rearrange("p c g -> p g c"), Ab.unsqueeze(2).to_broadcast([128, NG, NCH]), ALU.mult)
            self.cp(ahi, dta, "dve")
            self.tt(alo, dta, ahi, ALU.subtract)
            if KLVL <= 3:
                continue
            assert NCH * 32 <= 1024
            for c in range(NCH):
                pb = PS[2 + (c * 32) // 512]
                o = (c * 32) % 512
                self.mm(pb[:, o:o + 8], self.triu[:], dta[:, c, 0:8])
                self.mm(pb[:, o + 8:o + 16], self.tril[:], dta[:, c, 8:16])
                self.mm(pb[:, o + 16:o + 32], self.onesf[:], dta[:, c, :])
            if KLVL <= 4:
                continue
            for c0 in range(0, NCH, 16):
                cn = min(16, NCH - c0)
                pb = PS[2 + c0 // 16]
                pv = pb[:, 0:cn * 32].rearrange("p (c g) -> p c g", g=32)
                self.tt(bia[:, c0:c0 + cn, :], tmpB[:, c0:c0 + cn, :], pv[:, :, 0:16], ALU.subtract)
                self.act(ecu[:, c0:c0 + cn, :], pv[:, :, 0:16], AF.Exp)
                self.tt(wen[:, c0:c0 + cn, :], bia[:, c0:c0 + cn, :], pv[:, :, 16:32], ALU.add)
                self.act(eto[:, c0:c0 + cn, :], pv[:, :, 16:32], AF.Exp)
            self.act(fl(wen), fl(wen), AF.Exp)
            if "ssd1" in KDBG:
                continue
            for (dst, c0, ci_) in ((BT, OFF_B + g * 128, 16 + g), (CT, OFF_C + g * 128, 20 + g)):
                w = self.wload(self.wview(W, c0, 128), KD, 128)
                self.conv1d(w, cpad, acc, self.scw[:, ci_, :], self.scb[:, ci_:ci_ + 1], lambda t0, n, dst=dst: dst[:, t0:t0 + n], AF.Silu)
            for j in range(4):
                w = self.wload(self.wview(W, OFF_X + g * 512 + j * 128, 128), KD, 128)
                xc = xTc[j % 2]
                self.conv1d(w, cpad, acc, self.scw[:, g * 4 + j, :], self.scb[:, g * 4 + j:g * 4 + j + 1], lambda t0, n, xc=xc: xc[:, t0:t0 + n], AF.Silu)
                for c in range(NCH):
                    ps = PS[6 + c % 2]
                    self.mm(ps[:, 0:128], xc[:, c * 128:(c + 1) * 128], self.identb[:])
                    self.cp(xtok[:, c, j * 128:(j + 1) * 128], ps[:, 0:128], "dve" if c % 2 else "act")
            for j in range(4):
                w = self.wload(self.wview(W, OFF_Z + g * 512 + j * 128, 128), KD, 128)
                self.cp(wz[:, :, j * 128:(j + 1) * 128], w, "pool")
            self.dma(Dg, d["sdrep"][0:1, g * 512:(g + 1) * 512].to_broadcast([128, 512]))
            self.dma(ngt, d["sng"][0:1, g * 512:(g + 1) * 512].to_broadcast([128, 512]))
            yi = [0]

            def stageA(dr, c, slot):
                trib = self.mskf if dr == 0 else self.mskb
                nm = self.nmf if dr == 0 else self.nmb
                tk = slice(c * 128, (c + 1) * 128)
                self.mm(PS[0][:, 0:128], BT[:, tk], CT[:, tk])
                self.cp(CBTb[slot], PS[0][:, 0:128])
                for h in range(8):
                    pb = PS[2 + h // 4]
                    oo = (h % 4) * 128
                    self.mm(pb[:, oo:oo + 128], ahi[:, c, dr * 8 + h:dr * 8 + h + 1].to_broadcast([128, 128]), trib[:], start=True, stop=False)
                    self.mm(pb[:, oo:oo + 128], alo[:, c, dr * 8 + h:dr * 8 + h + 1].to_broadcast([128, 128]), trib[:], start=False, stop=False)
                    self.mm(pb[:, oo:oo + 128], self.identb[:], nm[:], start=False, stop=True)
                for hb_ in range(2):
                    pv = PS[2 + hb_][:, 0:512].rearrange("p (h t) -> p h t", t=128)
                    self.tt(pv, pv, bia[:, c, dr * 8 + hb_ * 4:dr * 8 + hb_ * 4 + 4].unsqueeze(2).to_broadcast([128, 4, 128]), ALU.add)
                    self.act(Ebuf[:, hb_ * 4:hb_ * 4 + 4, :], pv, AF.Exp)
                for h in range(8):
                    self.tt(Wt[slot][:, h, :], Ebuf[:, h, :], CBTb[slot], ALU.mult)

            def stageS(dr, c, sl):
                tk = slice(c * 128, (c + 1) * 128)
                self.tt(xhs[sl].rearrange("p (h q) -> p h q", q=64), xtok[:, c, :].rearrange("p (h q) -> p h q", q=64),
                        wen[:, c, dr * 8:(dr + 1) * 8].unsqueeze(2).to_broadcast([128, 8, 64]), ALU.mult, eng="pool")
                self.mm(PS[0][:, 128:256], BT[:, tk], self.identb[:])
                self.cp(Btks[sl], PS[0][:, 128:256])

            def stageB(dr, ci, c, slot, sl):
                tk = slice(c * 128, (c + 1) * 128)
                is_ctx = c >= NCL
                upd = ci < NCH - 1
                if upd:
                    self.tt(H32.rearrange("p (h q) -> p h q", q=64), H32.rearrange("p (h q) -> p h q", q=64),
                            eto[:, c, dr * 8:(dr + 1) * 8].unsqueeze(2).to_broadcast([128, 8, 64]), ALU.mult, eng="pool")
                if not is_ctx:
                    self.mm(PS[5][:, 0:512], CT[:, tk], Hb)
                    if dr == 1:
                        for k in range(KD):
                            self.mm(PS[1][:, 0:512], self.hT[:, k, tk], wz[:, k, :], start=(k == 0), stop=(k == KD - 1))
                        self.act(szb, PS[1][:, 0:512], AF.Silu)
                if upd:
                    self.mm(PS[7][:, 0:512], Btks[sl], xhs[sl])
                    self.tt(H32, H32, PS[7][:, 0:512], ALU.add)
                    self.cp(Hb, H32)
                if not is_ctx:
                    wt = Wt[slot]
                    for h in range(8):
                        self.mm(PS[4][:, h * 64:(h + 1) * 64], wt[:, h, :], xtok[:, c, h * 64:(h + 1) * 64])
                    self.tt(ytmp.rearrange("p (h q) -> p h q", q=64), PS[5][:, 0:512].rearrange("p (h q) -> p h q", q=64),
                            ecu[:, c, dr * 8:(dr + 1) * 8].unsqueeze(2).to_broadcast([128, 8, 64]), ALU.mult)
                    yf = yfs[yi[0] % 2]
                    yi[0] += 1
                    if dr == 0:
                        self.tt(yf, PS[4][:, 0:512], ytmp, ALU.add)
                        self.dma(self.scr_yf[c], yf)
                    else:
                        self.dma(yf, self.scr_yf[c])
                        self.tt(ysum, PS[4][:, 0:512], ytmp, ALU.add)
                        self.tt(ysum, ysum, yf, ALU.add)
                        self.tt(ydx, xtok[:, c, :], Dg, ALU.mult, eng="pool")
                        self.tt(ysum, ysum, ydx, ALU.add)
                        self.tt(ysum, ysum, szb, ALU.mult)
                        self.act(ytmp, ysum, AF.Square, accum_out=ssq)
                        self.act(ssq, ssq, AF.Ln, bias=self.epsD[:, 0:1], scale=1.0 / 512.0)
                        self.act(ssq, ssq, AF.Exp, scale=-0.5)
                        self.stt(ynbs[sl], ysum, ssq, ngt, ALU.mult, ALU.mult)
            def stageT(c, sl):
                for j in range(4):
                    self.mm(PS[6][:, j * 128:(j + 1) * 128], ynbs[sl][:, j * 128:(j + 1) * 128], self.identb[:])
                self.cp(ygT[:, c % 4, :], PS[6][:, 0:512])
                if c % 4 == 0:
                    cbase = c
                    ncs = min(4, NCL - cbase)
                    for f in range(KD):
                        w = self.wload(self.wview(d["s_w_out"], f * 128, 128, k0=g * 4, K=4), 4, 128)
                        ps = PS[f % 2]
                        for j in range(4):
                            rhs = ygT[:, 0:ncs, j * 128:(j + 1) * 128]
                            self.mm(ps[:, 0:ncs * 128].rearrange("p (c t) -> p c t", t=128), w[:, j, :], rhs, start=(j == 0), stop=(j == 3))
                        self.xt_update(ps[:, 0:ncs * 128], f, cbase * 128, ncs * 128, g1(f))

            for dr in range(2):
                seq = self.chunk_order(dr)
                lat = [c for c in seq if c < NCL]
                self.S.call("pool", "memset", ap=H32, constant=0.0)
                self.S.call("pool", "memset", ap=Hb, constant=0.0)
                stageA(dr, lat[0], 0)
                stageS(dr, seq[0], 0)
                li = 0
                for ci, c in enumerate(seq):
                    if ci + 1 < len(seq) - 1 or (ci + 1 < len(seq) and False):
                        stageS(dr, seq[ci + 1], (ci + 1) % 2)
                    if c < NCL:
                        if li + 1 < len(lat):
                            stageA(dr, lat[li + 1], (li + 1) % 2)
                        stageB(dr, ci, c, li % 2, ci % 2)
                        if dr == 1 and li >= 1:
                            stageT(lat[li - 1], (ci - 1) % 2)
                        li += 1
                    else:
                        stageB(dr, ci, c, 0, ci % 2)
                if dr == 1:
                    stageT(lat[-1], (len(seq) - 1) % 2)

    def build(self):
        self.alloc()
        self.epsD = self.sb("epsD", [128, 1], F32)
        self.lnq = self.sb("lnq", [128, 1], F32)
        self.lnk = self.sb("lnk", [128, 1], F32)
        self.S.call("pool", "memset", ap=self.epsD[:], constant=EPS)
        self.S.call("pool", "memset", ap=self.lnq[:], constant=math.log(0.125))
        self.S.call("pool", "memset", ap=self.lnk[:], constant=math.log(128.0 ** -0.5))
        self.consts()
        self.mod_phase()
        stop = self.stop
        order = ["norm", "l0mix", "l0ffn", "l1mix", "full"]
        lvl = order.index(stop)
        for b in range(self.nb):
            for k in range(KD):
                self.dma(self.XT[:, k, :], self.d["xT"][b, k * 128:(k + 1) * 128, :])
            for l in range(2):
                if lvl < 1 + 2 * l:
                    break
                A1 = lambda k, isc, l=l: self.Amix[:, l, k, (2 if isc else b):(3 if isc else b + 1)]
                B1 = lambda k, isc, l=l: self.modT[:, l, k, (2 if isc else b):(3 if isc else b + 1)]
                self.norm(A1, B1)
                if l == 0:
                    self.even_mixer(l, b)
                else:
                    self.ssd_mixer(l, b)
                if lvl < 2 + 2 * l:
                    break
                A2 = lambda k, isc, l=l: self.Affn[:, l, k, (2 if isc else b):(3 if isc else b + 1)]
                B2 = lambda k, isc, l=l: self.modT[:, l, 24 + k, (2 if isc else b):(3 if isc else b + 1)]
                self.norm(A2, B2, ctx=(l == 0))
                self.ffn(l, b, do_ctx=(l == 0))
            ost = self.scrf[:, 1536:1536 + 2 * 512].rearrange("p (i n) -> p i n", n=512)
            cnt = [0]

            def outbuf(k, t0, n):
                return ost[:, k % 2, 0:n]
            tl = self.tiles(True, False)
            for ti, (t0, n, isc) in enumerate(tl):
                ps = self.ps(ti % 2)
                for k in range(KD):
                    sq = self.sqb[:, k % 2, 0:n]
                    self.act(sq, self.XT[:, k, t0:t0 + n], AF.Square)
                    self.mm(ps[:, 0:n], self.onesb[:], sq, start=(k == 0), stop=(k == KD - 1))
                rs = self.rsb[:, 0, 0:n]
                self.act(rs, ps[:, 0:n], AF.Ln, bias=self.epsD[:, 0:1], scale=1.0 / D)
                self.act(ps[:, 0:n], rs, AF.Exp, scale=-0.5)
                for k in range(KD):
                    tmp = ost[:, k % 2, 0:n]
                    self.stt(tmp, self.XT[:, k, t0:t0 + n], self.gfin[:, k:k + 1], ps[:, 0:n], ALU.mult, ALU.mult)
                    self.out_ops.append(self.dma(self.outT[b, k * 128:(k + 1) * 128, t0:t0 + n], tmp))
        self.S.lower(self.out_ops)
        self.st.close()
        return self.nc


def _pl(v, nch):
    return np.ascontiguousarray(np.asarray(v, np.float32).reshape(nch, 128).T)


def prep_shared(inp):
    f = lambda a: np.ascontiguousarray(np.asarray(a, np.float32))
    sh = {}
    sh["mod_w"] = f(inp["mod_w"])
    sh["modb"] = np.ascontiguousarray(np.stack([_pl(inp["mod_b"][l], 48) for l in range(2)], axis=1))
    sh["gmix"] = np.ascontiguousarray(np.stack([_pl(inp["norm_mix_g"][l], 8) for l in range(2)], axis=1))
    sh["gffn"] = np.ascontiguousarray(np.stack([_pl(inp["norm_ffn_g"][l], 8) for l in range(2)], axis=1))
    sh["gfin"] = _pl(inp["final_norm_g"], 8)
    sh["w_up"] = f(inp["ffn_w_up"])
    cw = np.asarray(inp["ffn_conv_w"], np.float32).reshape(2, 9, 44, 128)
    sh["cw"] = np.ascontiguousarray(cw.transpose(3, 0, 2, 1))
    sh["cb"] = np.ascontiguousarray(np.asarray(inp["ffn_conv_b"], np.float32).reshape(2, 44, 128).transpose(2, 0, 1))
    sh["w_down"] = f(inp["ffn_w_down"])
    sh["e_w_in"] = f(inp["even_w_in"][0])
    sh["a1"] = np.ascontiguousarray(np.concatenate([inp["gla_a1"][0, 0], inp["gla_a1"][0, 1]], axis=1).astype(np.float32))
    sh["a2"] = np.ascontiguousarray(np.asarray(inp["gla_a2"][0], np.float32).transpose(1, 0, 2))
    sh["ab"] = np.ascontiguousarray(np.asarray(inp["gla_ab"][0], np.float32).reshape(2, 2, 128).transpose(2, 0, 1))
    sh["mlcw"] = np.ascontiguousarray(np.asarray(inp["ml_conv_w"][0], np.float32).reshape(3, 8, 128).transpose(2, 1, 0))
    sh["mlcb"] = _pl(inp["ml_conv_b"][0], 8)
    sh["gateb"] = f(inp["ml_gate_b"][0]).reshape(1, 16)
    sh["glang"] = _pl(inp["gla_norm_g"][0], 4)
    sh["mlng"] = _pl(inp["ml_norm_g"][0], 4)
    sh["e_w_out"] = f(inp["even_w_out"][0])
    sh["s_w_in"] = f(inp["ssd_w_in"][0])
    sh["scw"] = np.ascontiguousarray(np.asarray(inp["ssd_conv_w"][0], np.float32).reshape(3, 24, 128).transpose(2, 1, 0))
    sh["scb"] = _pl(inp["ssd_conv_b"][0], 24)
    sh["sdtb"] = np.ascontiguousarray(np.asarray(inp["ssd_dt_bias"][0], np.float32).reshape(2, 4, 8).transpose(1, 0, 2).reshape(4, 16))
    sh["salog"] = np.ascontiguousarray(np.asarray(inp["ssd_a_log"][0], np.float32).reshape(2, 4, 8).transpose(1, 0, 2).reshape(4, 16))
    sh["sdrep"] = np.ascontiguousarray(np.repeat(np.asarray(inp["ssd_d"][0], np.float32), 64).reshape(1, 2048))
    sh["sng"] = f(inp["ssd_norm_g"][0]).reshape(1, 2048)
    sh["s_w_out"] = f(inp["ssd_w_out"][0])
    sh["c_identf"] = np.eye(128, dtype=np.float32)
    sh["c_triu"] = np.triu(np.ones((128, 128), np.float32))
    sh["c_tril"] = np.tril(np.ones((128, 128), np.float32))
    return sh


def core_inputs(inp, sh, b0, nb):
    x = np.asarray(inp["x"], np.float32)
    ctx = np.asarray(inp["ctx"], np.float32)
    m = dict(sh)
    xt = np.concatenate([x[b0:b0 + nb], ctx[b0:b0 + nb]], axis=1)
    m["xT"] = np.ascontiguousarray(xt.transpose(0, 2, 1))
    cs = np.concatenate([np.asarray(inp["c"], np.float32)[b0:b0 + nb], np.zeros((2 - nb, D), np.float32) if nb < 2 else np.zeros((0, D), np.float32),
                         np.asarray(inp["c_ctx"], np.float32)[None]], axis=0)
    m["cT"] = np.ascontiguousarray(cs.reshape(3, KD, 128).transpose(2, 1, 0))
    return m


_CACHE = {}


def run(inp, n_cores=8, nb=2, RB=16, stop="full", sim=False):
    TL = inp["x"].shape[1]
    TC = inp["ctx"].shape[1]
    key = (TL, TC, RB, stop, nb)
    nc = Builder(TL, TC, RB, stop, nb).build()
    sh = prep_shared(inp)
    in_maps = [core_inputs(inp, sh, i * nb, nb) for i in range(n_cores)]
    if sim:
        from simrun import sim_run
        res = sim_run(nc, in_maps)
    else:
        res = run_bass_kernel_spmd(nc, in_maps, core_ids=list(range(n_cores))).results
    out = np.concatenate([r["outT"] for r in res], axis=0)
    return np.ascontiguousarray(out.transpose(0, 2, 1))


def kernel(**inputs):
    return run(inputs, n_cores=8, nb=2, RB=16, stop="full").astype(np.float32)
```
